# Optimizing a Trainium2 kernel written in Bass

```python
import jax, jax.numpy as jnp
from jax import lax
import numpy as np

D_MODEL = 2048
BATCH = 4
SEQ = 8192
DEPTH = 1

D_MIX = D_MODEL
D_GMLP = D_MIX // 2
D_LRU = D_MIX - D_GMLP
CHUNK = 128
GMLP_HEAD_DIM = 128
N_GMLP_HEADS = D_GMLP // GMLP_HEAD_DIM
LRU_BLOCK = 128
N_LRU_BLOCKS = D_LRU // LRU_BLOCK
CONV_WIDTH = 4
LRU_C = 8.0
D_PLE = 256
EPS = 1e-6
D_IN_PROJ = 3 * D_GMLP + 2 * D_LRU

kernel_name = "hymba_gmlp_rglru_sandwich_ple"


def rmsnorm(x, g):
    xf = x.astype(jnp.float32)
    y = xf * lax.rsqrt(jnp.mean(xf * xf, axis=-1, keepdims=True) + EPS)
    return (y * g.astype(jnp.float32)).astype(x.dtype)


def layernorm(x, g, b):
    xf = x.astype(jnp.float32)
    mu = jnp.mean(xf, axis=-1, keepdims=True)
    xc = xf - mu
    y = xc * lax.rsqrt(jnp.mean(xc * xc, axis=-1, keepdims=True) + EPS)
    return (y * g.astype(jnp.float32) + b.astype(jnp.float32)).astype(x.dtype)


def gmlp_branch(u, v, ln_g, ln_b, w_s, b_s):
    bsz, s, _ = v.shape
    u = jax.nn.gelu(u)
    v = layernorm(jax.nn.gelu(v), ln_g, ln_b)
    vc = v.reshape(bsz, s // CHUNK, CHUNK, N_GMLP_HEADS, GMLP_HEAD_DIM)
    causal = jnp.tril(jnp.ones((CHUNK, CHUNK), dtype=bool))
    w = jnp.where(causal[None], w_s, jnp.zeros_like(w_s))
    mixed = jnp.einsum('hts,bcshd->bcthd', w, vc) + jnp.transpose(b_s)[None, None, :, :, None]
    return u * mixed.reshape(bsz, s, D_GMLP)


def _lin_rec_combine(left, right):
    a_l, b_l = left
    a_r, b_r = right
    return a_l * a_r, a_r * b_l + b_r


def rglru_branch(xb, conv_w, conv_b, w_a, b_a, w_x, b_x, lam):
    bsz, s, c = xb.shape
    xc = lax.conv_general_dilated(
        xb, conv_w, window_strides=(1,), padding=[(CONV_WIDTH - 1, 0)],
        dimension_numbers=('NWC', 'WIO', 'NWC'), feature_group_count=c) + conv_b
    xh = xc.reshape(bsz, s, N_LRU_BLOCKS, LRU_BLOCK)
    r = jax.nn.sigmoid(jnp.einsum('bshi,hij->bshj', xh, w_a) + b_a).reshape(bsz, s, c)
    i = jax.nn.sigmoid(jnp.einsum('bshi,hij->bshj', xh, w_x) + b_x).reshape(bsz, s, c)
    log_a = -LRU_C * r.astype(jnp.float32) * jax.nn.softplus(-lam.astype(jnp.float32))
    a = jnp.exp(log_a)
    mult = jnp.sqrt(-jnp.expm1(2.0 * log_a))
    is_first = (jnp.arange(s) == 0)[None, :, None]
    mult = jnp.where(is_first, jnp.ones_like(mult), mult)
    bt = mult * (i * xc).astype(jnp.float32)
    _, h = lax.associative_scan(_lin_rec_combine, (a, bt), axis=1)
    return h.astype(xb.dtype)


def setup_inputs(seed: int = 0) -> dict:
    key = jax.random.key(seed)
    ks = jax.random.split(key, 24)
    f32 = jnp.float32
    n = lambda k, shape, scale: jax.random.normal(k, shape, f32) * scale
    gain = lambda k, shape: 1.0 + 0.01 * jax.random.normal(k, shape, f32)
    x = jax.random.normal(ks[0], (BATCH, SEQ, D_MODEL), f32)
    p = jax.random.normal(ks[1], (DEPTH, BATCH, SEQ, D_PLE), f32)
    pre_g = gain(ks[2], (DEPTH, D_MODEL))
    w_in = n(ks[3], (DEPTH, D_MODEL, D_IN_PROJ), D_MODEL ** -0.5)
    gmlp_ln_g = gain(ks[4], (DEPTH, D_GMLP))
    gmlp_ln_b = n(ks[5], (DEPTH, D_GMLP), 0.01)
    gmlp_ws = n(ks[6], (DEPTH, N_GMLP_HEADS, CHUNK, CHUNK), CHUNK ** -0.5)
    gmlp_bs = gain(ks[7], (DEPTH, N_GMLP_HEADS, CHUNK))
    conv_w = n(ks[8], (DEPTH, CONV_WIDTH, 1, D_LRU), CONV_WIDTH ** -0.5)
    conv_b = n(ks[9], (DEPTH, D_LRU), 0.01)
    w_a = n(ks[10], (DEPTH, N_LRU_BLOCKS, LRU_BLOCK, LRU_BLOCK), LRU_BLOCK ** -0.5)
    b_a = n(ks[11], (DEPTH, N_LRU_BLOCKS, LRU_BLOCK), 0.01)
    w_x = n(ks[12], (DEPTH, N_LRU_BLOCKS, LRU_BLOCK, LRU_BLOCK), LRU_BLOCK ** -0.5)
    b_x = n(ks[13], (DEPTH, N_LRU_BLOCKS, LRU_BLOCK), 0.01)
    a0 = jax.random.uniform(ks[14], (DEPTH, D_LRU), f32, 0.9, 0.999)
    s0 = a0 ** (1.0 / LRU_C)
    lam = jnp.log(s0) - jnp.log1p(-s0)
    gmlp_out_g = gain(ks[15], (DEPTH, D_GMLP))
    lru_out_g = gain(ks[16], (DEPTH, D_LRU))
    w_out = n(ks[17], (DEPTH, D_MIX, D_MODEL), D_MIX ** -0.5)
    post_g = gain(ks[18], (DEPTH, D_MODEL))
    w_pe = n(ks[19], (DEPTH, D_PLE, D_MODEL), D_PLE ** -0.5)
    w_pg = n(ks[20], (DEPTH, D_MODEL, D_MODEL), D_MODEL ** -0.5)
    return {"x": x, "p": p, "pre_g": pre_g, "w_in": w_in, "gmlp_ln_g": gmlp_ln_g,
            "gmlp_ln_b": gmlp_ln_b, "gmlp_ws": gmlp_ws, "gmlp_bs": gmlp_bs,
            "conv_w": conv_w, "conv_b": conv_b, "w_a": w_a, "b_a": b_a, "w_x": w_x,
            "b_x": b_x, "lam": lam, "gmlp_out_g": gmlp_out_g, "lru_out_g": lru_out_g,
            "w_out": w_out, "post_g": post_g, "w_pe": w_pe, "w_pg": w_pg}


def reference(x, p, pre_g, w_in, gmlp_ln_g, gmlp_ln_b, gmlp_ws, gmlp_bs, conv_w, conv_b,
              w_a, b_a, w_x, b_x, lam, gmlp_out_g, lru_out_g, w_out, post_g, w_pe, w_pg):
    h = x
    splits = [D_GMLP, 2 * D_GMLP, 3 * D_GMLP, 3 * D_GMLP + D_LRU]
    for l in range(DEPTH):
        hn = rmsnorm(h, pre_g[l])
        z = hn @ w_in[l]
        u, v, gate_a, xb, gate_b = jnp.split(z, splits, axis=-1)
        ya = gmlp_branch(u, v, gmlp_ln_g[l], gmlp_ln_b[l], gmlp_ws[l], gmlp_bs[l]) * jax.nn.silu(gate_a)
        yb = rglru_branch(xb, conv_w[l], conv_b[l], w_a[l], b_a[l], w_x[l], b_x[l], lam[l]) * jax.nn.silu(gate_b)
        y = jnp.concatenate([rmsnorm(ya, gmlp_out_g[l]), rmsnorm(yb, lru_out_g[l])], axis=-1)
        h = h + rmsnorm(y @ w_out[l], post_g[l])
        h = h + (p[l] @ w_pe[l]) * jax.nn.sigmoid(h @ w_pg[l])
    return h
```

```python
import numpy as np
import concourse.bass as bass
import concourse.mybir as mybir
from contextlib import ExitStack

F32 = mybir.dt.float32
BF16 = mybir.dt.bfloat16
AF = mybir.ActivationFunctionType
ALU = mybir.AluOpType
AX = mybir.AxisListType


class Res:
    __slots__ = ("name", "w", "r", "excl")

    def __init__(self, name, excl=False):
        self.name = name
        self.excl = excl
        self.w = None
        self.r = []


class Prog:
    ENGS = ("pe", "act", "dve", "pool", "sp")

    def __init__(self, nc, es, same_engine_waits=True):
        self.nc = nc
        self.es = es
        self.same = same_engine_waits
        self.sem = {e: es.enter_context(nc.semaphore("s_" + e)) for e in self.ENGS}
        self.cnt = {e: 0 for e in self.ENGS}
        self.ops = {e: [] for e in self.ENGS}
        self.known = {e: {} for e in self.ENGS}
        self.dsem = {}
        self.dcnt = {}
        self.semname = {}
        for e in self.ENGS:
            self.semname[id(self.sem[e])] = e

    def res(self, name, excl=False):
        return Res(name, excl)

    def _deps(self, eng, rd, wr):
        deps = {}
        def add(tok):
            if tok is None:
                return
            k = id(tok[0])
            if k not in deps or deps[k][1] < tok[1]:
                deps[k] = tok
        for r in rd:
            add(r.w)
        for w in wr:
            add(w.w)
            for t in w.r:
                add(t)
        waits = []
        own = id(self.sem[eng])
        for k, tok in deps.items():
            if k == own and (eng == "pe" or not self.same):
                continue
            if self.known[eng].get(k, 0) < tok[1]:
                self.known[eng][k] = tok[1]
                waits.append(tok)
        return waits

    def _commit(self, tok, rd, wr):
        for r in rd:
            if r not in wr:
                r.r.append(tok)
        for w in wr:
            w.w = tok
            w.r = []

    def op(self, eng, fn, rd=(), wr=()):
        ex = [r for r in rd if r.excl and r not in wr]
        if ex:
            wr = list(wr) + ex
        waits = self._deps(eng, rd, wr)
        self.cnt[eng] += 1
        tok = (self.sem[eng], self.cnt[eng])
        self.ops[eng].append((waits, fn, (self.sem[eng], 1)))
        self._commit(tok, rd, wr)
        return tok

    def dma(self, eng, key, out, in_, rd=(), wr=(), **kw):
        if key not in self.dsem:
            self.dsem[key] = self.es.enter_context(self.nc.semaphore("d_" + key))
            self.dcnt[key] = 0
        waits = self._deps(eng, rd, wr)
        self.dcnt[key] += 16
        sem = self.dsem[key]
        tok = (sem, self.dcnt[key])
        self.ops[eng].append((waits, lambda e: e.dma_start(out=out, in_=in_, **kw), (sem, 16)))
        self._commit(tok, rd, wr)
        return tok

    def wait_all(self, eng, toks):
        waits = []
        for tok in toks:
            k = id(tok[0])
            if self.known[eng].get(k, 0) < tok[1]:
                self.known[eng][k] = tok[1]
                waits.append(tok)
        self.ops[eng].append((waits, None, None))

    def emit(self):
        nc = self.nc

        def replay(name):
            def f(e):
                for waits, fn, inc in self.ops[name]:
                    for (sem, val) in waits:
                        e.wait_ge(sem, val)
                    if fn is not None:
                        ins = fn(e)
                        if inc is not None:
                            ins.then_inc(inc[0], inc[1])
            return f

        with nc.Block() as block:
            block.tensor(replay("pe"))
            block.scalar(replay("act"))
            block.vector(replay("dve"))
            block.gpsimd(replay("pool"))
            block.sync(replay("sp"))


D = 2048
DG = 1024
DIN = 5120
T = 512
NT = 4
KC = 16
EPS = 1e-6
NCST = 106
C_PREG, C_CW, C_CB, C_BA, C_BX, C_LAM, C_GA, C_GB, C_HASPREV, C_ISFIRST, C_LNB = 0, 16, 48, 56, 64, 72, 80, 88, 96, 97, 98
R_LNG, R_LNB, R_POSTG, R_BS, R_PREG, NROW = 0, 1024, 2048, 4096, 5120, 7168


def build_program(nc, NCH, do_prepass=True, dbg=None):
    NTOK = NCH * T
    es = ExitStack()
    with es, nc.allow_low_precision("bf16 matmul operands with fp32 accumulation"):
        P = Prog(nc, es)

        def din(name, shape, dt=F32):
            return nc.dram_tensor(name, shape, dt, kind="ExternalInput").ap()

        x_d = din("x", [NTOK, D])
        xp_d = din("xprev", [NTOK, D])
        p_d = din("p", [NTOK, 256])
        win_d = din("w_in", [D, DIN])
        wout_d = din("w_out", [D, D])
        wpg_d = din("w_pg", [D, D])
        wpe_d = din("w_pe", [256, D])
        wa_d = din("w_a", [8, 128, 128])
        wx_d = din("w_x", [8, 128, 128])
        ws_d = din("gmlp_ws", [8, 128, 128])
        cst_d = din("cst", [128, NCST])
        row_d = din("rowv", [1, NROW])
        id_d = din("ident", [128, 128])
        mk_d = din("mask", [128, 128])
        out_d = nc.dram_tensor("out", [NTOK, D], F32, kind="ExternalOutput").ap()
        wsc_v = nc.dram_tensor("wsc_v", [2, 128, KC, 512], BF16).ap()
        wsc_h = nc.dram_tensor("wsc_h", [16, 128, KC, 256], BF16).ap()
        HS = [0, 1, 4, 5, 6, 7, 8, 9]
        hidx = lambda s_, half: HS.index(s_) * 2 + half
        wsc_out = nc.dram_tensor("wsc_out", [4, 128, KC, 512], BF16).ap()
        wsc_pg = nc.dram_tensor("wsc_pg", [4, 128, KC, 512], BF16).ap()

        def sb(name, shape, dt=F32):
            return es.enter_context(nc.sbuf_tensor("sb_" + name, shape, dt))

        cst = sb("cst", [128, NCST]); R_cst = P.res("cst")
        ident = sb("ident", [128, 128]); R_id = P.res("ident")
        mask = sb("mask", [128, 128]); R_mask = P.res("mask")
        ones_b = sb("ones_b", [128, 128], BF16); R_ob = P.res("ones_b")
        g_bc = sb("g_bc", [128, 1024]); R_gbc = P.res("g_bc")
        pg_bc = sb("pg_bc", [128, 2048]); R_pgbc = P.res("pg_bc")
        chl = sb("chl", [128, 2, 8, 128], BF16); R_chl = P.res("chl")
        ident_b = sb("ident_b", [128, 128], BF16); R_idb = P.res("ident_b")
        preg_bc = sb("preg_bc", [128, 2048]); R_pregbc = P.res("preg_bc")
        wsT = sb("wsT", [128, 8, 128], BF16); R_wsT = P.res("wsT")
        wa_sb = sb("wa_sb", [128, 8, 128], BF16); R_wa = P.res("wa")
        wx_sb = sb("wx_sb", [128, 8, 128], BF16); R_wx = P.res("wx")
        wpe_sb = sb("wpe_sb", [128, 2, 2048], BF16); R_wpe = P.res("wpe")
        der = sb("der", [128, 48]); R_der = P.res("der")
        cc = sb("cc", [128, 4]); R_cc = P.res("cc")
        hst = sb("hst", [128, 8]); R_hst = [P.res(f"hst{j}") for j in range(8)]
        halo = sb("halo", [128, 8, 3]); R_halo = [P.res(f"halo{j}") for j in range(8)]
        small = sb("small", [128, 64]); R_small = {}
        xld = [sb(f"xld{i}", [128, D]) for i in range(2)]; R_xld = [P.res(f"xld{i}") for i in range(2)]
        xsb = [sb(f"xsb{i}", [128, D], BF16) for i in range(NT)]; R_xsb = [P.res(f"xsb{i}") for i in range(NT)]
        xT = sb("xT", [128, KC, T], BF16); R_xT = [P.res(f"xT{k}") for k in range(KC)]
        slab = [sb(f"slab{i}", [128, KC, 512], BF16) for i in range(2)]; R_slab = [P.res(f"slab{i}") for i in range(2)]
        xbs = [sb(f"xbs{i}", [128, 520]) for i in range(4)]; R_xbs = [P.res(f"xbs{i}") for i in range(4)]
        yab = sb("yab", [128, 8, 512]); R_yab = [P.res(f"yab{j}") for j in range(8)]
        ynT = sb("ynT", [128, KC, T], BF16); R_ynT = [P.res(f"ynT{k}") for k in range(KC)]
        sqb = [sb(f"sqb{i}", [128, 512], BF16) for i in range(2)]; R_sqb = [P.res(f"sqb{i}") for i in range(2)]
        arena = sb("arena", [128, 20 * 512]); R_ar = [P.res(f"ar{i}") for i in range(20)]

        def ar(slot, n=1):
            return arena[:, slot * 512:(slot + n) * 512]

        def ar_bf(slot, n=1):
            return arena[:, slot * 512:(slot + n) * 512].bitcast(BF16)

        psb = [es.enter_context(nc.psum_tensor(f"ps{i}", [128, 512], F32)) for i in range(8)]
        R_ps = [P.res(f"ps{i}", excl=True) for i in range(8)]
        rot = {"i": 0}

        live_banks = set()

        def ps_next(hold=False):
            while True:
                i = rot["i"] % 6
                rot["i"] += 1
                if i not in live_banks:
                    break
            if hold:
                live_banks.add(i)
            return psb[i], R_ps[i]

        def ps_release(pt):
            for i in range(6):
                if psb[i] is pt:
                    live_banks.discard(i)

        slab_i = {"i": 0}

        def load_slab(src):
            i = slab_i["i"] % 2
            slab_i["i"] += 1
            P.dma("sp", f"slab{i}", slab[i][:], src, rd=[src_res[id(src)]] if id(src) in src_res else [], wr=[R_slab[i]])
            return slab[i], R_slab[i]

        src_res = {}

        P.dma("sp", "c0", cst[:], cst_d[:, :], wr=[R_cst])
        P.dma("sp", "c1", ident[:], id_d[:, :], wr=[R_id])
        P.dma("sp", "c2", mask[:], mk_d[:, :], wr=[R_mask])
        P.dma("sp", "c3", g_bc[:], row_d[0:1, R_LNG:R_LNG + 1024].partition_broadcast(128) if False else row_d[0:1, R_LNG:R_LNG + 1024].to_broadcast([128, 1024]), wr=[R_gbc])
        P.dma("sp", "c4", pg_bc[:], row_d[0:1, R_POSTG:R_POSTG + 2048].to_broadcast([128, 2048]), wr=[R_pgbc])
        P.op("dve", lambda e: e.memset(ones_b[:], 1.0), wr=[R_ob])
        P.dma("sp", "c8", preg_bc[:], row_d[0:1, R_PREG:R_PREG + 2048].to_broadcast([128, 2048]), wr=[R_pregbc])
        P.op("dve", lambda e: e.tensor_copy(out=ident_b[:], in_=ident[:]), rd=[R_id], wr=[R_idb])
        P.op("dve", lambda e: e.memset(cc[:, 0:1], -0.5), wr=[R_cc])
        P.op("dve", lambda e: e.memset(cc[:, 1:2], 0.5), wr=[R_cc])
        P.op("dve", lambda e: e.memset(cc[:, 2:3], EPS), wr=[R_cc])
        P.op("dve", lambda e: e.memset(cc[:, 3:4], 0.25), wr=[R_cc])
        P.op("dve", lambda e: e.memset(hst[:], 0.0), wr=R_hst)
        P.op("dve", lambda e: e.memset(halo[:], 0.0), wr=R_halo)

        R_win = [P.res(f"wsc_in{s}") for s in range(10)]
        R_winh = {(s_, h_): P.res(f"wsc_h{s_}_{h_}") for s_ in range(10) for h_ in range(2)}
        R_wout = [P.res(f"wsc_out{s}") for s in range(4)]
        R_wpg = [P.res(f"wsc_pg{s}") for s in range(4)]
        win_v = win_d.rearrange("(kc p) (s f) -> s p kc f", p=128, f=512)
        wout_v = wout_d.rearrange("(kc p) (s f) -> s p kc f", p=128, f=512)
        wpg_v = wpg_d.rearrange("(kc p) (s f) -> s p kc f", p=128, f=512)

        conv_toks = []

        def throttle():
            if len(conv_toks) >= 2:
                P.wait_all("pool", [conv_toks[-2]])

        win_h = win_d.rearrange("(kc p) (s a f) -> s a p kc f", p=128, a=2, f=256)

        def conv_in(s):
            if s in (2, 3):
                throttle()
                conv_toks.append(P.dma("pool", f"cv_in{s}", wsc_v[s - 2], win_v[s], wr=[R_win[s]]))
            else:
                for half in range(2):
                    throttle()
                    conv_toks.append(P.dma("pool", f"cv_in{s}_{half}", wsc_h[hidx(s, half)], win_h[s][half], wr=[R_winh[(s, half)]]))

        def conv_out(s):
            throttle()
            conv_toks.append(P.dma("pool", f"cv_out{s}", wsc_out[s], wout_v[s], wr=[R_wout[s]]))

        def conv_pg(s):
            throttle()
            conv_toks.append(P.dma("pool", f"cv_pg{s}", wsc_pg[s], wpg_v[s], wr=[R_wpg[s]]))

        import os
        SKIP = os.environ.get('MK_SKIP', '').split(',')
        if 'conv' not in SKIP:
            conv_in(6); conv_in(7)
        if 'wa' not in SKIP: throttle(); conv_toks.append(P.dma("pool", "cv_wa", wa_sb[:], wa_d.rearrange("h i j -> i h j"), wr=[R_wa]))
        if 'wa' not in SKIP: throttle(); conv_toks.append(P.dma("pool", "cv_wx", wx_sb[:], wx_d.rearrange("h i j -> i h j"), wr=[R_wx]))
        pending_conv = [lambda s=s: conv_in(s) for s in (8, 9, 0, 1, 2, 3, 4, 5)]
        pending_conv += [lambda: (throttle(), conv_toks.append(P.dma("pool", "cv_wpe", wpe_sb[:], wpe_d.rearrange("(kc p) f -> p kc f", p=128), wr=[R_wpe])))]
        pending_conv += [lambda s=s: conv_out(s) for s in range(4)] + [lambda s=s: conv_pg(s) for s in range(4)]

        def slab_src(kind, s):
            if kind == "in":
                assert s in (2, 3)
                return wsc_v[s - 2], R_win[s]
            if kind == "out":
                return wsc_out[s], R_wout[s]
            return wsc_pg[s], R_wpg[s]

        def get_slab(kind, s):
            src, rsrc = slab_src(kind, s)
            i = slab_i["i"] % 2
            slab_i["i"] += 1
            P.dma("sp", f"slab{i}", slab[i][:], src, rd=[rsrc], wr=[R_slab[i]])
            return slab[i], R_slab[i]

        def get_slab2(parts):
            i = slab_i["i"] % 2
            slab_i["i"] += 1
            view = slab[i][:].rearrange("p (a k2) f -> p a (k2 f)", a=2).rearrange("p a (k f) -> p a k f", f=256)
            for a_, (sidx, half) in enumerate(parts):
                P.dma("sp", f"slab{i}", view[:, a_, :, :], wsc_h[hidx(sidx, half)], rd=[R_winh[(sidx, half)]], wr=[R_slab[i]])
            return view, R_slab[i]

        def proj_fm2(view, rsl, a_, col, src, rsrc, hold=False):
            pt, rp = ps_next(hold)
            def f(e, pt=pt):
                ins = None
                for kc in range(KC):
                    ins = e.matmul(pt[:], lhsT=view[:, a_, kc, col:col + 128], rhs=src[:, kc, :], start=(kc == 0), stop=(kc == KC - 1))
                return ins
            P.op("pe", f, rd=[rsl] + rsrc, wr=[rp])
            return pt, rp

        if 'der' not in SKIP:
            P.op("dve", lambda e: e.tensor_scalar(out=der[:, 0:8], in0=cst[:, C_BA:C_BA + 8], scalar1=0.5, scalar2=None, op0=ALU.mult), rd=[R_cst], wr=[R_der])
            P.op("dve", lambda e: e.tensor_scalar(out=der[:, 8:16], in0=cst[:, C_BX:C_BX + 8], scalar1=0.5, scalar2=None, op0=ALU.mult), rd=[R_cst], wr=[R_der])
            lam = cst[:, C_LAM:C_LAM + 8]
            t0, t1, t2, t3 = der[:, 32:40], der[:, 40:48], small[:, 0:8], small[:, 8:16]
            R_sm = P.res("small")
            P.op("dve", lambda e: e.tensor_scalar(out=t0, in0=lam, scalar1=-1.0, scalar2=None, op0=ALU.mult), rd=[R_cst], wr=[R_der])
            P.op("dve", lambda e: e.tensor_tensor(out=t0, in0=t0, in1=lam, op=ALU.max), rd=[R_cst, R_der], wr=[R_der])
            P.op("act", lambda e: e.activation(out=t0, in_=t0, func=AF.Exp, scale=-1.0), rd=[R_der], wr=[R_der])
            P.op("dve", lambda e: e.tensor_scalar(out=t1, in0=t0, scalar1=1.0, scalar2=None, op0=ALU.add), rd=[R_der], wr=[R_der])
            P.op("act", lambda e: e.activation(out=t2, in_=t1, func=AF.Ln), rd=[R_der], wr=[R_sm])
            P.op("dve", lambda e: e.tensor_scalar(out=t1, in0=t1, scalar1=-1.0, scalar2=1e-30, op0=ALU.add, op1=ALU.max), rd=[R_der], wr=[R_der])
            P.op("dve", lambda e: e.reciprocal(out=t1, in_=t1), rd=[R_der], wr=[R_der])
            P.op("dve", lambda e: e.tensor_tensor(out=t1, in0=t1, in1=t0, op=ALU.mult), rd=[R_der], wr=[R_der])
            P.op("dve", lambda e: e.tensor_tensor(out=t2, in0=t2, in1=t1, op=ALU.mult), rd=[R_der, R_sm], wr=[R_sm])
            P.op("dve", lambda e: e.tensor_scalar(out=t3, in0=lam, scalar1=-1.0, scalar2=0.0, op0=ALU.mult, op1=ALU.max), rd=[R_cst], wr=[R_sm])
            P.op("dve", lambda e: e.tensor_tensor(out=t2, in0=t2, in1=t3, op=ALU.add), rd=[R_sm], wr=[R_sm])
            P.op("dve", lambda e: e.tensor_scalar(out=der[:, 16:24], in0=t2, scalar1=-8.0, scalar2=None, op0=ALU.mult), rd=[R_sm], wr=[R_der])
            P.op("dve", lambda e: e.tensor_scalar(out=der[:, 24:32], in0=t2, scalar1=-4.0, scalar2=None, op0=ALU.mult), rd=[R_sm], wr=[R_der])
            HBA, HBX, KF, KH = 0, 8, 16, 24

        if 'ws' not in SKIP:
            ws_f = arena[:, 0:1024].rearrange("p (h s) -> p h s", h=8)
            ws_b = ar_bf(2, 1).rearrange("p (h s) -> p h s", h=8)
            bs_bc = arena[:, 4 * 512:6 * 512]
            c32 = arena[:, 6 * 512:8 * 512]
            P.dma("sp", "c7", ws_f, ws_d.rearrange("h t s -> t h s"), wr=[R_ar[0], R_ar[1]])
            P.dma("sp", "c6", bs_bc, row_d[0:1, R_BS:R_BS + 1024].to_broadcast([128, 1024]), wr=[R_ar[4], R_ar[5]])
            for h in range(8):
                P.op("dve", lambda e, h=h: e.tensor_tensor(out=ws_b[:, h, :], in0=ws_f[:, h, :], in1=mask[:], op=ALU.mult),
                     rd=[R_mask, R_ar[0], R_ar[1]], wr=[R_ar[2]])
            for h2 in range(2):
                pt, rp = ps_next()
                ptb = pt[:].bitcast(BF16)
                def f(e, h2=h2, ptb=ptb):
                    ins = None
                    for q in range(4):
                        h = h2 * 4 + q
                        ins = e.transpose(out=ptb[:, q * 128:(q + 1) * 128], in_=ws_b[:, h, :], identity=ident_b[:])
                    return ins
                P.op("pe", f, rd=[R_ar[2], R_idb], wr=[rp])
                P.op("dve", lambda e, h2=h2, ptb=ptb: e.tensor_copy(out=wsT[:, h2 * 4:(h2 + 1) * 4, :], in_=ptb[:, 0:512].rearrange("p (h t) -> p h t", h=4)), rd=[rp], wr=[R_wsT])
            for h2 in range(2):
                pt2, rp2 = ps_next()
                P.op("pe", lambda e, h2=h2, pt2=pt2: e.matmul(pt2[:], lhsT=ones_b[:], rhs=wsT[:, h2 * 4:(h2 + 1) * 4, :].rearrange("p h t -> p (h t)"), start=True, stop=True),
                     rd=[R_ob, R_wsT], wr=[rp2])
                for q in range(4):
                    h = h2 * 4 + q
                    P.op("dve", lambda e, h=h, q=q, pt2=pt2: e.scalar_tensor_tensor(out=c32[:, h * 128:(h + 1) * 128], in0=pt2[:, q * 128:(q + 1) * 128], scalar=cst[:, C_LNB + h:C_LNB + h + 1],
                                                                         in1=bs_bc[:, h * 128:(h + 1) * 128], op0=ALU.mult, op1=ALU.add),
                         rd=[rp2, R_cst, R_ar[4], R_ar[5]], wr=[R_ar[6], R_ar[7]])
            chi = chl[:, 0, :, :].rearrange("p h t -> p (h t)")
            clo = chl[:, 1, :, :].rearrange("p h t -> p (h t)")
            P.op("dve", lambda e: e.tensor_copy(out=chi, in_=c32), rd=[R_ar[6], R_ar[7]], wr=[R_chl])
            P.op("dve", lambda e: e.tensor_tensor(out=clo, in0=c32, in1=chi, op=ALU.subtract), rd=[R_ar[6], R_ar[7], R_chl], wr=[R_chl])

        def rsqrt_pool(out_ap, in_ap, n, rd, wr):
            P.op("pool", lambda e: e.tensor_tensor(out=out_ap, in0=in_ap, in1=cc[:, 0:1].to_broadcast([128, n]), op=ALU.pow), rd=list(rd) + [R_cc], wr=wr)

        R_ss = P.res("ss4")
        ss4, ms4, rstd4 = small[:, 16:20], small[:, 20:24], small[:, 24:28]

        def transpose_to(tt, srcb, rsrc, dstbuf, Rdst):
            for half in range(2):
                pt, rp = ps_next()
                ptb = pt[:].bitcast(BF16)
                def ft(e, half=half, ptb=ptb):
                    ins = None
                    for q in range(8):
                        kc = half * 8 + q
                        ins = e.transpose(out=ptb[:, q * 128:(q + 1) * 128], in_=srcb[:, kc * 128:(kc + 1) * 128], identity=ident_b[:])
                    return ins
                P.op("pe", ft, rd=rsrc + [R_idb], wr=[rp])
                dst = dstbuf[:, half * 8:(half + 1) * 8, tt * 128:(tt + 1) * 128]
                srcv = ptb.rearrange("p (k t) -> p k t", k=8)
                wr = [Rdst[half * 8 + q] for q in range(8)]
                if half == 0:
                    P.op("act", lambda e, dst=dst, srcv=srcv: e.activation(out=dst, in_=srcv, func=AF.Copy), rd=[rp], wr=wr)
                else:
                    P.op("dve", lambda e, dst=dst, srcv=srcv: e.tensor_copy(out=dst, in_=srcv), rd=[rp], wr=wr)

        R_ssx = [P.res(f"ssx{tt}") for tt in range(NT)]

        def stage_xa(src_d, c):
            def load(tt):
                r0 = c * T + tt * 128
                P.dma("pool", f"xl{tt % 2}", xld[tt % 2][:], src_d[r0:r0 + 128, :], wr=[R_xld[tt % 2]])
            def norm(tt):
                P.op("act", lambda e: e.activation(out=xsb[tt][:], in_=xld[tt % 2][:], func=AF.Square, accum_out=ss4[:, tt:tt + 1]),
                     rd=[R_xld[tt % 2]], wr=[R_xsb[tt], R_ssx[tt]])
                P.op("dve", lambda e: e.tensor_scalar(out=ms4[:, tt:tt + 1], in0=ss4[:, tt:tt + 1], scalar1=1.0 / D, scalar2=EPS, op0=ALU.mult, op1=ALU.add), rd=[R_ssx[tt]], wr=[R_ssx[tt]])
                rsqrt_pool(rstd4[:, tt:tt + 1], ms4[:, tt:tt + 1], 1, [R_ssx[tt]], [R_ssx[tt]])
                P.op("dve", lambda e: e.scalar_tensor_tensor(out=xsb[tt][:], in0=xld[tt % 2][:], scalar=rstd4[:, tt:tt + 1], in1=preg_bc[:], op0=ALU.mult, op1=ALU.mult),
                     rd=[R_xld[tt % 2], R_ssx[tt], R_pregbc], wr=[R_xsb[tt]])
            return [lambda: (load(0), load(1), norm(0)), lambda: (load(2), norm(1)), lambda: (load(3), norm(2)), lambda: norm(3)]

        def run_steps(steps):
            for st in steps:
                st()

        def stage_xb(dstbuf, Rdst, tts=(0, 1, 2, 3)):
            for tt in tts:
                transpose_to(tt, xsb[tt][:], [R_xsb[tt]], dstbuf, Rdst)

        def proj_fm(sl, rsl, col, src, rsrc, hold=False):
            pt, rp = ps_next(hold)
            def f(e, pt=pt):
                ins = None
                for kc in range(KC):
                    ins = e.matmul(pt[:], lhsT=sl[:, kc, col:col + 128], rhs=src[:, kc, :], start=(kc == 0), stop=(kc == KC - 1))
                return ins
            P.op("pe", f, rd=[rsl] + rsrc, wr=[rp])
            return pt, rp

        NSET = 4

        def lset(s):
            b = 5 * s
            return dict(xcb=ar_bf(b + 0)[:, 0:512], tr=ar(b + 1), tiw=ar(b + 2), a=ar(b + 3), a2m=ar(b + 4),
                        Rxcb=R_ar[b + 0], Rtr=R_ar[b + 1], Rtiw=R_ar[b + 2], Ra=R_ar[b + 3], Ra2=R_ar[b + 4], xb=xbs[s], Rxb=R_xbs[s])

        def lru_front_a(j, s, xb_pt, xb_rp, act_heavy=False, phase=0):
            L = lset(s)
            xb, Rxb, xcb, Rxcb = L["xb"], L["Rxb"], L["xcb"], L["Rxcb"]
            cw = lambda k: cst[:, C_CW + j * 4 + k:C_CW + j * 4 + k + 1]
            if phase in (0, 1):
                P.op("dve", lambda e: e.tensor_copy(out=xb[:, 0:3], in_=halo[:, j, :]), rd=[R_halo[j]], wr=[Rxb])
                if act_heavy:
                    P.op("act", lambda e: e.activation(out=xb[:, 3:515], in_=xb_pt[:], func=AF.Copy), rd=[xb_rp], wr=[Rxb])
                else:
                    P.op("dve", lambda e: e.tensor_copy(out=xb[:, 3:515], in_=xb_pt[:]), rd=[xb_rp], wr=[Rxb])
                P.op("dve", lambda e: e.tensor_copy(out=halo[:, j, :], in_=xb[:, 512:515]), rd=[Rxb], wr=[R_halo[j]])
                P.op("act", lambda e: e.activation(out=xb_pt[:], in_=xb_pt[:], func=AF.Identity, scale=cw(3), bias=cst[:, C_CB + j:C_CB + j + 1]), rd=[xb_rp, R_cst, Rxb], wr=[xb_rp])
                for k in (0, 1, 2):
                    P.op("dve", lambda e, k=k: e.scalar_tensor_tensor(out=xb_pt[:], in0=xb[:, k:k + 512], scalar=cw(k), in1=xb_pt[:], op0=ALU.mult, op1=ALU.add),
                         rd=[Rxb, R_cst, xb_rp], wr=[xb_rp])
            if phase in (0, 2):
                if act_heavy:
                    P.op("act", lambda e: e.activation(out=xcb, in_=xb_pt[:], func=AF.Copy), rd=[xb_rp], wr=[Rxcb])
                else:
                    P.op("dve", lambda e: e.tensor_copy(out=xcb, in_=xb_pt[:]), rd=[xb_rp], wr=[Rxcb])

        def lru_front_b(j, s, xb_pt, xb_rp):
            L = lset(s)
            xcb, tr, tiw, a_t, a2m = L["xcb"], L["tr"], L["tiw"], L["a"], L["a2m"]
            Rxcb, Rtr, Rtiw, Ra, Ra2 = L["Rxcb"], L["Rtr"], L["Rtiw"], L["Ra"], L["Ra2"]
            zr, rzr = ps_next()
            P.op("pe", lambda e: e.matmul(zr[:], lhsT=wa_sb[:, j, :], rhs=xcb, start=True, stop=True), rd=[R_wa, Rxcb], wr=[rzr])
            zi, rzi = ps_next()
            P.op("pe", lambda e: e.matmul(zi[:], lhsT=wx_sb[:, j, :], rhs=xcb, start=True, stop=True), rd=[R_wx, Rxcb], wr=[rzi])
            col = lambda base: der[:, base + j:base + j + 1]
            P.op("act", lambda e: e.activation(out=tr, in_=zr[:], func=AF.Tanh, scale=0.5, bias=col(HBA)), rd=[rzr, R_der], wr=[Rtr])
            P.op("act", lambda e: e.activation(out=tiw, in_=zi[:], func=AF.Tanh, scale=0.5, bias=col(HBX)), rd=[rzi, R_der], wr=[Rtiw])
            P.op("dve", lambda e: e.scalar_tensor_tensor(out=tiw, in0=tiw, scalar=1.0, in1=xb_pt[:], op0=ALU.add, op1=ALU.mult), rd=[Rtiw, xb_rp], wr=[Rtiw])
            P.op("act", lambda e: e.activation(out=a_t, in_=tr, func=AF.Exp, scale=col(KH), bias=col(KH)), rd=[Rtr, R_der], wr=[Ra])
            P.op("act", lambda e: e.activation(out=a2m, in_=tr, func=AF.Exp, scale=col(KF), bias=col(KF)), rd=[Rtr, R_der], wr=[Ra2])
            P.op("dve", lambda e: e.tensor_scalar(out=a2m, in0=a2m, scalar1=0.99999994, scalar2=None, op0=ALU.min), rd=[Ra2], wr=[Ra2])

        def lru_back(j, s, first_mode, to_yab):
            L = lset(s)
            tr, tiw, a_t, a2m = L["tr"], L["tiw"], L["a"], L["a2m"]
            Rtr, Rtiw, Ra, Ra2 = L["Rtr"], L["Rtiw"], L["Ra"], L["Ra2"]
            P.op("act", lambda e: e.activation(out=a2m, in_=a2m, func=AF.Sqrt, scale=-0.25, bias=cc[:, 3:4]), rd=[Ra2, R_cc], wr=[Ra2])
            if first_mode == "force":
                P.op("dve", lambda e: e.memset(a2m[:, 0:1], 0.5), rd=[Ra2], wr=[Ra2])
            elif first_mode == "flag":
                tmpc = small[:, 32 + j:33 + j]
                P.op("dve", lambda e: e.tensor_scalar(out=tmpc, in0=a2m[:, 0:1], scalar1=-1.0, scalar2=0.5, op0=ALU.mult, op1=ALU.add), rd=[Ra2], wr=[R_sm])
                P.op("dve", lambda e: e.scalar_tensor_tensor(out=a2m[:, 0:1], in0=tmpc, scalar=cst[:, C_ISFIRST:C_ISFIRST + 1], in1=a2m[:, 0:1], op0=ALU.mult, op1=ALU.add),
                     rd=[Ra2, R_sm, R_cst], wr=[Ra2])
            P.op("dve", lambda e: e.tensor_tensor(out=tiw, in0=tiw, in1=a2m, op=ALU.mult), rd=[Rtiw, Ra2], wr=[Rtiw])
            if to_yab:
                h_t, Rh = yab[:, j, :], R_yab[j]
            else:
                h_t, Rh = tr, Rtr
            P.op("dve", lambda e: e.tensor_tensor_scan(out=h_t, data0=a_t, data1=tiw, initial=hst[:, j:j + 1], op0=ALU.mult, op1=ALU.add),
                 rd=[Ra, Rtiw, R_hst[j]], wr=[Rh])
            P.op("dve", lambda e: e.tensor_copy(out=hst[:, j:j + 1], in_=h_t[:, 511:512]), rd=[Rh], wr=[R_hst[j]])

        def norm_and_pack(ss_pt, ss_rp, gcol0, k0):
            rb = ss_pt[:]
            P.op("act", lambda e: e.activation(out=ss_pt[:], in_=ss_pt[:], func=AF.Sqrt, scale=1.0 / DG, bias=cc[:, 2:3]), rd=[ss_rp, R_cc], wr=[ss_rp])
            P.op("dve", lambda e: e.reciprocal(out=ss_pt[:], in_=ss_pt[:]), rd=[ss_rp], wr=[ss_rp])
            for j in range(8):
                P.op("dve", lambda e, j=j: e.scalar_tensor_tensor(out=ynT[:, k0 + j, :], in0=yab[:, j, :], scalar=cst[:, gcol0 + j:gcol0 + j + 1], in1=rb, op0=ALU.mult, op1=ALU.mult),
                     rd=[R_yab[j], R_cst, ss_rp], wr=[R_ynT[k0 + j]])

        def stage_lru(c, prepass, src, Rsrc, mid_hook=None):
            first_mode = None
            if c == 0:
                first_mode = "force" if prepass else "flag"
            slabs = {}
            ssb, rssb = psb[6], R_ps[6]
            deferred = []
            def flush():
                while deferred:
                    deferred.pop(0)()
            groups = [(0, 1, 2, 3), (4, 5, 6, 7)] if prepass else [(0, 1), (2, 3), (4, 5), (6, 7)]
            for pr, tiles in enumerate(groups):
                xbp = {}
                gbs = {}
                if prepass:
                    for j in tiles:
                        if j % 4 == 0:
                            slabs["xb"] = get_slab2([(6 + j // 4, 0), (6 + j // 4, 1)])
                        view, rsl = slabs["xb"]
                        xbp[j] = proj_fm2(view, rsl, (j % 4) // 2, (j % 2) * 128, src, Rsrc, hold=True)
                        lru_front_a(j, j % NSET, *xbp[j], act_heavy=True, phase=1)
                    for j in tiles:
                        lru_front_a(j, j % NSET, *xbp[j], act_heavy=True, phase=2)
                else:
                    view, rsl = get_slab2([(6 + pr // 2, pr % 2), (8 + pr // 2, pr % 2)])
                    for j in tiles:
                        xbp[j] = proj_fm2(view, rsl, 0, (j % 2) * 128, src, Rsrc, hold=True)
                        lru_front_a(j, j % NSET, *xbp[j])
                    for j in tiles:
                        gbs[j] = proj_fm2(view, rsl, 1, (j % 2) * 128, src, Rsrc, hold=True)
                flush()
                for j in tiles:
                    lru_front_b(j, j % NSET, *xbp[j])
                    ps_release(xbp[j][0])
                for j in tiles:
                    lru_back(j, j % NSET, first_mode, not prepass)
                if not prepass:
                    for j in tiles:
                        gb_pt, gb_rp = gbs[j]
                        L = lset(j % NSET)
                        tg, Rtg = L["tr"], L["Rtr"]
                        s2 = j % 2
                        P.op("act", lambda e, tg=tg, gb_pt=gb_pt: e.activation(out=tg, in_=gb_pt[:], func=AF.Tanh, scale=0.5), rd=[gb_rp], wr=[Rtg])
                        P.op("dve", lambda e, tg=tg, gb_pt=gb_pt: e.scalar_tensor_tensor(out=gb_pt[:], in0=tg, scalar=1.0, in1=gb_pt[:], op0=ALU.add, op1=ALU.mult), rd=[Rtg, gb_rp], wr=[gb_rp])
                        P.op("dve", lambda e, j=j, gb_pt=gb_pt: e.scalar_tensor_tensor(out=yab[:, j, :], in0=yab[:, j, :], scalar=0.5, in1=gb_pt[:], op0=ALU.mult, op1=ALU.mult), rd=[R_yab[j], gb_rp], wr=[R_yab[j]])
                        if pr == 3:
                            P.op("act", lambda e, j=j, s2=s2: e.activation(out=sqb[s2][:], in_=yab[:, j, :], func=AF.Square), rd=[R_yab[j]], wr=[R_sqb[s2]])
                        else:
                            P.op("pool", lambda e, j=j, s2=s2: e.tensor_tensor(out=sqb[s2][:], in0=yab[:, j, :], in1=yab[:, j, :], op=ALU.mult), rd=[R_yab[j]], wr=[R_sqb[s2]])
                        ps_release(gb_pt)
                        deferred.append(lambda j=j, s2=s2: P.op("pe", lambda e: e.matmul(ssb[:], lhsT=ones_b[:], rhs=sqb[s2][:], start=(j == 0), stop=(j == 7)), rd=[R_ob, R_sqb[s2]], wr=[rssb]))
                if mid_hook is not None and pr == len(groups) // 2 - 1:
                    mid_hook()
            if prepass:
                flush()
                return None
            def finish():
                flush()
                norm_and_pack(ssb, rssb, C_GB, 8)
            return finish

        gm = {}

        def stage_gmlp_v(c, after_first=None):
            v_ln = ar_bf(4, 4).rearrange("p (t f) -> p t f", t=4)
            R_vln = [R_ar[4 + tt] for tt in range(4)]
            stats = small[:, 40:52]
            mv = small[:, 52:54]
            rv = small[:, 54:56]
            R_st = P.res("stats") if "st" not in R_small else R_small["st"]
            R_small["st"] = R_st
            slv = [get_slab("in", 2), get_slab("in", 3)]
            for tt in range(NT):
                for half in range(2):
                    sl, rsl = slv[half]
                    pt, rp = ps_next()
                    def f(e, pt=pt, sl=sl, tt=tt):
                        ins = None
                        for kc in range(KC):
                            ins = e.matmul(pt[:], lhsT=xT[:, kc, tt * 128:(tt + 1) * 128], rhs=sl[:, kc, :], start=(kc == 0), stop=(kc == KC - 1))
                        return ins
                    P.op("pe", f, rd=[rsl] + R_xT, wr=[rp])
                    gslot = (tt % 2) * 2 + half
                    P.op("act", lambda e, pt=pt, gslot=gslot: e.activation(out=ar(gslot), in_=pt[:], func=AF.Gelu_apprx_tanh), rd=[rp], wr=[R_ar[gslot]])
                g0 = (tt % 2) * 2
                gv = arena[:, g0 * 512:(g0 + 2) * 512]
                P.op("dve", lambda e, gv=gv: e.bn_stats(out=stats[:, 0:6], in_=gv[:, 0:512]), rd=[R_ar[g0]], wr=[R_st])
                P.op("dve", lambda e, gv=gv: e.bn_stats(out=stats[:, 6:12], in_=gv[:, 512:1024]), rd=[R_ar[g0 + 1]], wr=[R_st])
                P.op("dve", lambda e: e.bn_aggr(out=mv, in_=stats), rd=[R_st], wr=[R_st])
                P.op("dve", lambda e: e.tensor_scalar(out=rv[:, 0:1], in0=mv[:, 1:2], scalar1=EPS, scalar2=None, op0=ALU.add), rd=[R_st], wr=[R_st])
                rsqrt_pool(rv[:, 0:1], rv[:, 0:1], 1, [R_st], [R_st])
                P.op("dve", lambda e: e.scalar_tensor_tensor(out=rv[:, 1:2], in0=mv[:, 0:1], scalar=-1.0, in1=rv[:, 0:1], op0=ALU.mult, op1=ALU.mult), rd=[R_st], wr=[R_st])
                P.op("act", lambda e, gv=gv: e.activation(out=gv, in_=gv, func=AF.Identity, scale=rv[:, 0:1], bias=rv[:, 1:2]), rd=[R_st, R_ar[g0], R_ar[g0 + 1]], wr=[R_ar[g0], R_ar[g0 + 1]])
                P.op("dve", lambda e, gv=gv, tt=tt: e.tensor_tensor(out=v_ln[:, tt, :], in0=gv, in1=g_bc[:], op=ALU.mult), rd=[R_ar[g0], R_ar[g0 + 1], R_gbc], wr=[R_vln[tt]])
                if tt == 1 and after_first is not None:
                    after_first()
            gm['v_ln'] = v_ln; gm['R_vln'] = R_vln

        def stage_gmlp_heads(c, steps=()):
            steps = list(steps)
            v_ln, R_vln = gm['v_ln'], gm['R_vln']
            ssa, rssa = psb[7], R_ps[7]
            slabs = {}
            hdef = []
            def hproj(j):
                if j % 2 == 0:
                    slabs["p"] = get_slab2([(0 + j // 4, (j % 4) // 2), (4 + j // 4, (j % 4) // 2)])
                s = j % 2
                b = 8 + 4 * s
                tg, ug, sga2, usg = ar(b), ar(b + 1), ar(b + 2), ar(b + 3)
                Rtg, Rug, Rsga, Rusg = R_ar[b], R_ar[b + 1], R_ar[b + 2], R_ar[b + 3]
                view, rsl = slabs["p"]
                ga_pt, ga_rp = proj_fm2(view, rsl, 1, (j % 2) * 128, xT, R_xT)
                while hdef:
                    hdef.pop(0)()
                P.op("act", lambda e: e.activation(out=tg, in_=ga_pt[:], func=AF.Tanh, scale=0.5), rd=[ga_rp], wr=[Rtg])
                P.op("dve", lambda e: e.scalar_tensor_tensor(out=sga2, in0=tg, scalar=1.0, in1=ga_pt[:], op0=ALU.add, op1=ALU.mult), rd=[Rtg, ga_rp], wr=[Rsga])
                u_pt, u_rp = proj_fm2(view, rsl, 0, (j % 2) * 128, xT, R_xT)
                P.op("act", lambda e: e.activation(out=ug, in_=u_pt[:], func=AF.Gelu_apprx_tanh), rd=[u_rp], wr=[Rug])
                P.op("dve", lambda e: e.scalar_tensor_tensor(out=usg, in0=ug, scalar=0.5, in1=sga2, op0=ALU.mult, op1=ALU.mult), rd=[Rug, Rsga], wr=[Rusg])

            def hmix(j):
                s = j % 2
                b = 8 + 4 * s
                usg, Rusg = ar(b + 3), R_ar[b + 3]
                mp, rmp = ps_next()
                def fm(e):
                    ins = None
                    for cq in range(4):
                        e.matmul(mp[:, cq * 128:(cq + 1) * 128], lhsT=ident_b[:], rhs=chl[:, 0, j, :], start=True, stop=False)
                        e.matmul(mp[:, cq * 128:(cq + 1) * 128], lhsT=ident_b[:], rhs=chl[:, 1, j, :], start=False, stop=False)
                        ins = e.matmul(mp[:, cq * 128:(cq + 1) * 128], lhsT=v_ln[:, cq, j * 128:(j + 1) * 128], rhs=wsT[:, j, :], start=False, stop=True)
                    return ins
                P.op("pe", fm, rd=[R_chl, R_idb, R_wsT] + R_vln, wr=[rmp])
                P.op("dve", lambda e: e.tensor_tensor(out=yab[:, j, :], in0=mp[:], in1=usg, op=ALU.mult), rd=[rmp, Rusg], wr=[R_yab[j]])
                P.op("act", lambda e: e.activation(out=sqb[s][:], in_=yab[:, j, :], func=AF.Square), rd=[R_yab[j]], wr=[R_sqb[s]])
                if j % 2 == 1 and steps:
                    steps.pop(0)()
                hdef.append(lambda: P.op("pe", lambda e: e.matmul(ssa[:], lhsT=ones_b[:], rhs=sqb[s][:], start=(j == 0), stop=(j == 7)), rd=[R_ob, R_sqb[s]], wr=[rssa]))

            hproj(0)
            for j in range(8):
                if j + 1 < 8:
                    hproj(j + 1)
                hmix(j)
            while hdef:
                hdef.pop(0)()
            norm_and_pack(ssa, rssa, C_GA, 0)

        part = small[:, 56:60]
        R_po = P.res("po")
        po16 = sb("po16", [128, 16])
        sso, rso = sb("sso", [128, 4]), sb("rso", [128, 4])

        def o_sb(tt):
            if tt < 2:
                return yab[:, tt * 4:(tt + 1) * 4, :].rearrange("p a b -> p (a b)"), R_yab[tt * 4:(tt + 1) * 4]
            return arena[:, (tt - 2) * 2048:(tt - 1) * 2048], R_ar[(tt - 2) * 4:(tt - 1) * 4]

        def stage_out(c, mid=None):
            sqj = ar_bf(16, 1)[:, 0:512]
            res_load(c, 0, "sp"); res_load(c, 1, "sp")
            for nb in range(4):
                sl, rsl = get_slab("out", nb)
                pre = {}
                if nb == 0:
                    for tt in range(NT):
                        pt, rp = ps_next(hold=True)
                        def f0(e, pt=pt, sl=sl, tt=tt):
                            ins = None
                            for kc in range(8, KC):
                                ins = e.matmul(pt[:], lhsT=ynT[:, kc, tt * 128:(tt + 1) * 128], rhs=sl[:, kc, :], start=(kc == 8), stop=False)
                            return ins
                        P.op("pe", f0, rd=[rsl] + R_ynT[8:], wr=[rp])
                        pre[tt] = (pt, rp)
                for tt in range(NT):
                    if nb == 0:
                        pt, rp = pre[tt]
                        def f(e, pt=pt, sl=sl, tt=tt):
                            ins = None
                            for kc in range(8):
                                ins = e.matmul(pt[:], lhsT=ynT[:, kc, tt * 128:(tt + 1) * 128], rhs=sl[:, kc, :], start=False, stop=(kc == 7))
                            return ins
                        P.op("pe", f, rd=[rsl] + R_ynT[:8], wr=[rp])
                        ps_release(pt)
                    else:
                        pt, rp = ps_next()
                        def f(e, pt=pt, sl=sl, tt=tt):
                            ins = None
                            for kc in range(KC):
                                ins = e.matmul(pt[:], lhsT=ynT[:, kc, tt * 128:(tt + 1) * 128], rhs=sl[:, kc, :], start=(kc == 0), stop=(kc == KC - 1))
                            return ins
                        P.op("pe", f, rd=[rsl] + R_ynT, wr=[rp])
                    osb, Rosb = o_sb(tt)
                    P.op("dve", lambda e, pt=pt, osb=osb, nb=nb: e.tensor_tensor(out=osb[:, nb * 512:(nb + 1) * 512], in0=pt[:], in1=pg_bc[:, nb * 512:(nb + 1) * 512], op=ALU.mult), rd=[rp, R_pgbc], wr=[Rosb[nb]])
                    P.op("act", lambda e, pt=pt, tt=tt, nb=nb: e.activation(out=sqj, in_=pt[:], func=AF.Square, accum_out=po16[:, tt * 4 + nb:tt * 4 + nb + 1]), rd=[rp], wr=[R_ar[16], R_po])
            if mid is not None:
                mid()
            P.op("dve", lambda e: e.reduce_sum(out=sso[:], in_=po16[:].rearrange("p (t n) -> p t n", t=4), axis=AX.X), rd=[R_po], wr=[R_po])
            P.op("dve", lambda e: e.tensor_scalar(out=sso[:], in0=sso[:], scalar1=1.0 / D, scalar2=EPS, op0=ALU.mult, op1=ALU.add), rd=[R_po], wr=[R_po])
            rsqrt_pool(rso[:], sso[:], 4, [R_po], [R_po])

        def res_load(c, tt, eng):
            r0 = c * T + tt * 128
            P.dma(eng, f"xr{tt % 2}", xld[tt % 2][:], x_d[r0:r0 + 128, :], wr=[R_xld[tt % 2]])

        def stage_h1T(c):
            for tt in range(NT):
                s2 = 16 + 2 * (tt % 2)
                hb = ar_bf(s2, 2)
                rs_ = [R_ar[s2], R_ar[s2 + 1]]
                osb, Rosb = o_sb(tt)
                for hf in range(2):
                    cs = slice(hf * 1024, (hf + 1) * 1024)
                    P.op("dve", lambda e, osb=osb, tt=tt, cs=cs: e.scalar_tensor_tensor(out=osb[:, cs], in0=osb[:, cs], scalar=rso[:, tt:tt + 1], in1=xld[tt % 2][:, cs], op0=ALU.mult, op1=ALU.add),
                         rd=Rosb[2 * hf:2 * hf + 2] + [R_po, R_xld[tt % 2]], wr=Rosb[2 * hf:2 * hf + 2])
                if tt + 2 < NT:
                    res_load(c, tt + 2, "act")
                P.op("act", lambda e, osb=osb, hb=hb: e.activation(out=hb[:, 0:1024], in_=osb[:, 0:1024], func=AF.Copy), rd=Rosb[0:2], wr=[rs_[0]])
                P.op("dve", lambda e, osb=osb, hb=hb: e.tensor_copy(out=hb[:, 1024:2048], in_=osb[:, 1024:2048]), rd=Rosb[2:4], wr=[rs_[1]])
                transpose_to(tt, hb, rs_, ynT, R_ynT)

        def stage_final(c):
            p_sb = arena[:, 8 * 512:10 * 512].rearrange("p (t f) -> p t f", t=4)
            pT = ar_bf(10, 1).rearrange("p (k t) -> p k t", k=2)
            P.dma("sp", "pld", p_sb, p_d[c * T:(c + 1) * T, :].rearrange("(t p) f -> p t f", p=128), wr=[R_ar[8], R_ar[9]])
            p_b = ar_bf(15, 1).rearrange("p (t f) -> p t f", t=4)
            P.op("dve", lambda e: e.tensor_copy(out=p_b, in_=p_sb), rd=[R_ar[8], R_ar[9]], wr=[R_ar[15]])
            for kc in range(2):
                pt, rp = ps_next()
                ptb = pt[:].bitcast(BF16)
                def ft(e, kc=kc, ptb=ptb):
                    ins = None
                    for tt in range(NT):
                        ins = e.transpose(out=ptb[:, tt * 128:(tt + 1) * 128], in_=p_b[:, tt, kc * 128:(kc + 1) * 128], identity=ident_b[:])
                    return ins
                P.op("pe", ft, rd=[R_ar[15], R_idb], wr=[rp])
                P.op("act", lambda e, kc=kc, ptb=ptb: e.activation(out=pT[:, kc, :], in_=ptb[:, 0:512], func=AF.Copy), rd=[rp], wr=[R_ar[10]])
            toks = []
            for nb in range(4):
                sl, rsl = get_slab("pg", nb)
                for tt in range(NT):
                    s = tt % 2
                    tgf, q = ar(11 + s), ar(13 + s)
                    Rtgf, Rq = R_ar[11 + s], R_ar[13 + s]
                    gp, rgp = ps_next()
                    def f(e, gp=gp, sl=sl, tt=tt):
                        ins = None
                        for kc in range(KC):
                            ins = e.matmul(gp[:], lhsT=ynT[:, kc, tt * 128:(tt + 1) * 128], rhs=sl[:, kc, :], start=(kc == 0), stop=(kc == KC - 1))
                        return ins
                    P.op("pe", f, rd=[rsl] + R_ynT, wr=[rgp])
                    pp, rpp = ps_next()
                    def f2(e, pp=pp, tt=tt, nb=nb):
                        ins = None
                        for kc in range(2):
                            ins = e.matmul(pp[:], lhsT=pT[:, kc, tt * 128:(tt + 1) * 128], rhs=wpe_sb[:, kc, nb * 512:(nb + 1) * 512], start=(kc == 0), stop=(kc == 1))
                        return ins
                    P.op("pe", f2, rd=[R_wpe, R_ar[10]], wr=[rpp])
                    sl_ = slice(nb * 512, (nb + 1) * 512)
                    P.op("act", lambda e, gp=gp, tgf=tgf: e.activation(out=tgf, in_=gp[:], func=AF.Tanh, scale=0.5), rd=[rgp], wr=[Rtgf])
                    P.op("dve", lambda e, tgf=tgf, q=q, pp=pp: e.scalar_tensor_tensor(out=q, in0=tgf, scalar=1.0, in1=pp[:], op0=ALU.add, op1=ALU.mult), rd=[Rtgf, rpp], wr=[Rq])
                    osb, Rosb = o_sb(tt)
                    P.op("dve", lambda e, q=q, osb=osb, sl_=sl_: e.scalar_tensor_tensor(out=osb[:, sl_], in0=q, scalar=0.5, in1=osb[:, sl_], op0=ALU.mult, op1=ALU.add), rd=[Rq, Rosb[nb]], wr=[Rosb[nb]])
                    r0 = c * T + tt * 128
                    toks.append(P.dma("pool", f"st{tt}_{nb}", out_d[r0:r0 + 128, sl_], osb[:, sl_], rd=[Rosb[nb]]))
            return toks

        def run_pending(n):
            for _ in range(n):
                if pending_conv:
                    pending_conv.pop(0)()

        jobs = ([("pre", c) for c in range(NCH)] if do_prepass else []) + [("main", c) for c in range(NCH)]
        def buf_of(i):
            kind, c = jobs[i]
            if kind == "pre" and (NCH - 1 - c) % 2 == 0:
                return ynT, R_ynT
            return xT, R_xT
        def xa(i):
            kind, c = jobs[i]
            return stage_xa(xp_d if kind == "pre" else x_d, c)
        def xb_(i):
            stage_xb(*buf_of(i))
        all_toks = []
        run_steps(xa(0)); xb_(0)
        for i, (kind, c) in enumerate(jobs):
            nxt = i + 1 if i + 1 < len(jobs) else None
            src, Rsrc = buf_of(i)
            if kind == "pre":
                if nxt is not None:
                    run_steps(xa(nxt))
                stage_lru(c, True, src, Rsrc, mid_hook=(lambda nxt=nxt: xb_(nxt)) if nxt is not None else None)
                run_pending(3)
                if c == NCH - 1:
                    hp = cst[:, C_HASPREV:C_HASPREV + 1]
                    P.op("dve", lambda e: e.tensor_scalar(out=hst[:], in0=hst[:], scalar1=hp, scalar2=None, op0=ALU.mult), rd=R_hst + [R_cst], wr=R_hst)
                    P.op("dve", lambda e: e.tensor_scalar(out=halo[:].rearrange("p a b -> p (a b)"), in0=halo[:].rearrange("p a b -> p (a b)"), scalar1=hp, scalar2=None, op0=ALU.mult), rd=R_halo + [R_cst], wr=R_halo)
            else:
                run_pending(100)
                lru_finish = stage_lru(c, False, src, Rsrc)
                stage_gmlp_v(c, after_first=lru_finish)
                steps = xa(nxt) if nxt is not None else []
                stage_gmlp_heads(c, steps)
                if nxt is not None:
                    stage_xb(*buf_of(nxt), tts=(0, 1))
                stage_out(c, mid=(lambda nxt=nxt: stage_xb(*buf_of(nxt), tts=(2, 3))) if nxt is not None else None)
                stage_h1T(c)
                all_toks += stage_final(c)
        P.wait_all("sp", all_toks)
        P.emit()
    return nc


def pack_inputs(inp, NCH):
    x = inp["x"]; p = inp["p"][0]
    B, S, _ = x.shape
    NTOK = NCH * T
    assert S == 2 * NTOK
    f = lambda a: np.ascontiguousarray(a, dtype=np.float32)
    cst = np.zeros((128, NCST), np.float32)
    cst[:, C_PREG:C_PREG + 16] = inp["pre_g"][0].reshape(16, 128).T
    cw = inp["conv_w"][0][:, 0, :]
    cst[:, C_CW:C_CW + 32] = cw.reshape(4, 8, 128).transpose(2, 1, 0).reshape(128, 32)
    col8 = lambda v: v.reshape(8, 128).T
    cst[:, C_CB:C_CB + 8] = col8(inp["conv_b"][0])
    cst[:, C_BA:C_BA + 8] = col8(inp["b_a"][0].reshape(-1))
    cst[:, C_BX:C_BX + 8] = col8(inp["b_x"][0].reshape(-1))
    cst[:, C_LAM:C_LAM + 8] = col8(inp["lam"][0])
    cst[:, C_GA:C_GA + 8] = col8(inp["gmlp_out_g"][0])
    cst[:, C_GB:C_GB + 8] = col8(inp["lru_out_g"][0])
    rowv = np.zeros((1, NROW), np.float32)
    rowv[0, R_LNG:R_LNG + 1024] = inp["gmlp_ln_g"][0]
    rowv[0, R_LNB:R_LNB + 1024] = inp["gmlp_ln_b"][0]
    rowv[0, R_POSTG:R_POSTG + 2048] = inp["post_g"][0]
    rowv[0, R_BS:R_BS + 1024] = inp["gmlp_bs"][0].reshape(-1)
    rowv[0, R_PREG:R_PREG + 2048] = inp["pre_g"][0]
    cst[:, C_LNB:C_LNB + 8] = col8(inp["gmlp_ln_b"][0])
    common = {
        "w_in": f(inp["w_in"][0]), "w_out": f(inp["w_out"][0]), "w_pg": f(inp["w_pg"][0]), "w_pe": f(inp["w_pe"][0]),
        "w_a": f(inp["w_a"][0]), "w_x": f(inp["w_x"][0]), "gmlp_ws": f(inp["gmlp_ws"][0]),
        "rowv": rowv, "ident": np.eye(128, dtype=np.float32), "mask": np.tril(np.ones((128, 128), np.float32)),
    }
    maps = []
    for b in range(B):
        for half in range(2):
            c = cst.copy()
            c[:, C_HASPREV] = float(half)
            c[:, C_ISFIRST] = 1.0 - float(half)
            m = dict(common)
            m["cst"] = c
            m["x"] = f(x[b, half * NTOK:(half + 1) * NTOK])
            m["xprev"] = f(x[b, 0:NTOK]) if half == 1 else np.zeros((NTOK, D), np.float32)
            m["p"] = f(p[b, half * NTOK:(half + 1) * NTOK])
            maps.append(m)
    return maps

def unpack(results, B, NCH):
    NTOK = NCH * T
    out = np.zeros((B, 2 * NTOK, D), np.float32)
    for b in range(B):
        for half in range(2):
            out[b, half * NTOK:(half + 1) * NTOK] = results[b * 2 + half]["out"]
    return out


from concourse.bass_utils import run_bass_kernel_spmd

NCH_FULL = 8


def kernel(**inputs):
    inp = {k: np.asarray(v) for k, v in inputs.items()}
    B = inp["x"].shape[0]
    nc = bass.Bass("TRN2", target_bir_lowering=False)
    build_program(nc, NCH_FULL)
    maps = pack_inputs(inp, NCH_FULL)
    res = run_bass_kernel_spmd(nc, maps, core_ids=list(range(2 * B)))
    return unpack(res.results, B, NCH_FULL)
```

```python
import numpy as np
import concourse.bass as bass
import concourse.mybir as mybir
from contextlib import ExitStack

F32 = mybir.dt.float32
BF16 = mybir.dt.bfloat16
AF = mybir.ActivationFunctionType
ALU = mybir.AluOpType
AX = mybir.AxisListType


class Res:
    __slots__ = ("name", "w", "r", "excl")

    def __init__(self, name, excl=False):
        self.name = name
        self.excl = excl
        self.w = None
        self.r = []


class Prog:
    ENGS = ("pe", "act", "dve", "pool", "sp")

    def __init__(self, nc, es, same_engine_waits=True):
        self.nc = nc
        self.es = es
        self.same = same_engine_waits
        self.sem = {e: es.enter_context(nc.semaphore("s_" + e)) for e in self.ENGS}
        self.cnt = {e: 0 for e in self.ENGS}
        self.ops = {e: [] for e in self.ENGS}
        self.known = {e: {} for e in self.ENGS}
        self.dsem = {}
        self.dcnt = {}
        self.semname = {}
        for e in self.ENGS:
            self.semname[id(self.sem[e])] = e

    def res(self, name, excl=False):
        return Res(name, excl)

    def _deps(self, eng, rd, wr):
        deps = {}
        def add(tok):
            if tok is None:
                return
            k = id(tok[0])
            if k not in deps or deps[k][1] < tok[1]:
                deps[k] = tok
        for r in rd:
            add(r.w)
        for w in wr:
            add(w.w)
            for t in w.r:
                add(t)
        waits = []
        own = id(self.sem[eng])
        for k, tok in deps.items():
            if k == own and (eng == "pe" or not self.same):
                continue
            if self.known[eng].get(k, 0) < tok[1]:
                self.known[eng][k] = tok[1]
                waits.append(tok)
        return waits

    def _commit(self, tok, rd, wr):
        for r in rd:
            if r not in wr:
                r.r.append(tok)
        for w in wr:
            w.w = tok
            w.r = []

    def op(self, eng, fn, rd=(), wr=()):
        ex = [r for r in rd if r.excl and r not in wr]
        if ex:
            wr = list(wr) + ex
        waits = self._deps(eng, rd, wr)
        self.cnt[eng] += 1
        tok = (self.sem[eng], self.cnt[eng])
        self.ops[eng].append((waits, fn, (self.sem[eng], 1)))
        self._commit(tok, rd, wr)
        return tok

    def dma(self, eng, key, out, in_, rd=(), wr=(), **kw):
        if key not in self.dsem:
            self.dsem[key] = self.es.enter_context(self.nc.semaphore("d_" + key))
            self.dcnt[key] = 0
        waits = self._deps(eng, rd, wr)
        self.dcnt[key] += 16
        sem = self.dsem[key]
        tok = (sem, self.dcnt[key])
        self.ops[eng].append((waits, lambda e: e.dma_start(out=out, in_=in_, **kw), (sem, 16)))
        self._commit(tok, rd, wr)
        return tok

    def wait_all(self, eng, toks):
        waits = []
        for tok in toks:
            k = id(tok[0])
            if self.known[eng].get(k, 0) < tok[1]:
                self.known[eng][k] = tok[1]
                waits.append(tok)
        self.ops[eng].append((waits, None, None))

    def emit(self):
        nc = self.nc

        def replay(name):
            def f(e):
                for waits, fn, inc in self.ops[name]:
                    for (sem, val) in waits:
                        e.wait_ge(sem, val)
                    if fn is not None:
                        ins = fn(e)
                        if inc is not None:
                            ins.then_inc(inc[0], inc[1])
            return f

        with nc.Block() as block:
            block.tensor(replay("pe"))
            block.scalar(replay("act"))
            block.vector(replay("dve"))
            block.gpsimd(replay("pool"))
            block.sync(replay("sp"))


D = 2048
DG = 1024
DIN = 5120
T = 512
NT = 4
KC = 16
EPS = 1e-6
NCST = 106
C_PREG, C_CW, C_CB, C_BA, C_BX, C_LAM, C_GA, C_GB, C_HASPREV, C_ISFIRST, C_LNB = 0, 16, 48, 56, 64, 72, 80, 88, 96, 97, 98
R_LNG, R_LNB, R_POSTG, R_BS, R_PREG, NROW = 0, 1024, 2048, 4096, 5120, 7168


def build_program(nc, NCH, do_prepass=True, dbg=None):
    NTOK = NCH * T
    es = ExitStack()
    with es, nc.allow_low_precision("bf16 matmul operands with fp32 accumulation"):
        P = Prog(nc, es)

        def din(name, shape, dt=F32):
            return nc.dram_tensor(name, shape, dt, kind="ExternalInput").ap()

        x_d = din("x", [NTOK, D])
        xp_d = din("xprev", [NTOK, D])
        p_d = din("p", [NTOK, 256])
        win_d = din("w_in", [D, DIN])
        wout_d = din("w_out", [D, D])
        wpg_d = din("w_pg", [D, D])
        wpe_d = din("w_pe", [256, D])
        wa_d = din("w_a", [8, 128, 128])
        wx_d = din("w_x", [8, 128, 128])
        ws_d = din("gmlp_ws", [8, 128, 128])
        cst_d = din("cst", [128, NCST])
        row_d = din("rowv", [1, NROW])
        id_d = din("ident", [128, 128])
        mk_d = din("mask", [128, 128])
        out_d = nc.dram_tensor("out", [NTOK, D], F32, kind="ExternalOutput").ap()
        wsc_v = nc.dram_tensor("wsc_v", [2, 128, KC, 512], BF16).ap()
        wsc_h = nc.dram_tensor("wsc_h", [16, 128, KC, 256], BF16).ap()
        HS = [0, 1, 4, 5, 6, 7, 8, 9]
        hidx = lambda s_, half: HS.index(s_) * 2 + half
        wsc_out = nc.dram_tensor("wsc_out", [4, 128, KC, 512], BF16).ap()
        wsc_pg = nc.dram_tensor("wsc_pg", [4, 128, KC, 512], BF16).ap()

        def sb(name, shape, dt=F32):
            return es.enter_context(nc.sbuf_tensor("sb_" + name, shape, dt))

        cst = sb("cst", [128, NCST]); R_cst = P.res("cst")
        ident = sb("ident", [128, 128]); R_id = P.res("ident")
        mask = sb("mask", [128, 128]); R_mask = P.res("mask")
        ones_b = sb("ones_b", [128, 128], BF16); R_ob = P.res("ones_b")
        g_bc = sb("g_bc", [128, 1024]); R_gbc = P.res("g_bc")
        pg_bc = sb("pg_bc", [128, 2048]); R_pgbc = P.res("pg_bc")
        chl = sb("chl", [128, 2, 8, 128], BF16); R_chl = P.res("chl")
        ident_b = sb("ident_b", [128, 128], BF16); R_idb = P.res("ident_b")
        preg_bc = sb("preg_bc", [128, 2048]); R_pregbc = P.res("preg_bc")
        wsT = sb("wsT", [128, 8, 128], BF16); R_wsT = P.res("wsT")
        wa_sb = sb("wa_sb", [128, 8, 128], BF16); R_wa = P.res("wa")
        wx_sb = sb("wx_sb", [128, 8, 128], BF16); R_wx = P.res("wx")
        wpe_sb = sb("wpe_sb", [128, 2, 2048], BF16); R_wpe = P.res("wpe")
        der = sb("der", [128, 48]); R_der = P.res("der")
        cc = sb("cc", [128, 4]); R_cc = P.res("cc")
        hst = sb("hst", [128, 8]); R_hst = [P.res(f"hst{j}") for j in range(8)]
        halo = sb("halo", [128, 8, 3]); R_halo = [P.res(f"halo{j}") for j in range(8)]
        small = sb("small", [128, 64]); R_small = {}
        xld = [sb(f"xld{i}", [128, D]) for i in range(2)]; R_xld = [P.res(f"xld{i}") for i in range(2)]
        xsb = [sb(f"xsb{i}", [128, D], BF16) for i in range(NT)]; R_xsb = [P.res(f"xsb{i}") for i in range(NT)]
        xT = sb("xT", [128, KC, T], BF16); R_xT = [P.res(f"xT{k}") for k in range(KC)]
        slab = [sb(f"slab{i}", [128, KC, 512], BF16) for i in range(2)]; R_slab = [P.res(f"slab{i}") for i in range(2)]
        xbs = [sb(f"xbs{i}", [128, 520]) for i in range(4)]; R_xbs = [P.res(f"xbs{i}") for i in range(4)]
        yab = sb("yab", [128, 8, 512]); R_yab = [P.res(f"yab{j}") for j in range(8)]
        ynT = sb("ynT", [128, KC, T], BF16); R_ynT = [P.res(f"ynT{k}") for k in range(KC)]
        sqb = [sb(f"sqb{i}", [128, 512], BF16) for i in range(2)]; R_sqb = [P.res(f"sqb{i}") for i in range(2)]
        arena = sb("arena", [128, 20 * 512]); R_ar = [P.res(f"ar{i}") for i in range(20)]

        def ar(slot, n=1):
            return arena[:, slot * 512:(slot + n) * 512]

        def ar_bf(slot, n=1):
            return arena[:, slot * 512:(slot + n) * 512].bitcast(BF16)

        psb = [es.enter_context(nc.psum_tensor(f"ps{i}", [128, 512], F32)) for i in range(8)]
        R_ps = [P.res(f"ps{i}", excl=True) for i in range(8)]
        rot = {"i": 0}

        live_banks = set()

        def ps_next(hold=False):
            while True:
                i = rot["i"] % 6
                rot["i"] += 1
                if i not in live_banks:
                    break
            if hold:
                live_banks.add(i)
            return psb[i], R_ps[i]

        def ps_release(pt):
            for i in range(6):
                if psb[i] is pt:
                    live_banks.discard(i)

        slab_i = {"i": 0}

        def load_slab(src):
            i = slab_i["i"] % 2
            slab_i["i"] += 1
            P.dma("sp", f"slab{i}", slab[i][:], src, rd=[src_res[id(src)]] if id(src) in src_res else [], wr=[R_slab[i]])
            return slab[i], R_slab[i]

        src_res = {}

        P.dma("sp", "c0", cst[:], cst_d[:, :], wr=[R_cst])
        P.dma("sp", "c1", ident[:], id_d[:, :], wr=[R_id])
        P.dma("sp", "c2", mask[:], mk_d[:, :], wr=[R_mask])
        P.dma("sp", "c3", g_bc[:], row_d[0:1, R_LNG:R_LNG + 1024].partition_broadcast(128) if False else row_d[0:1, R_LNG:R_LNG + 1024].to_broadcast([128, 1024]), wr=[R_gbc])
        P.dma("sp", "c4", pg_bc[:], row_d[0:1, R_POSTG:R_POSTG + 2048].to_broadcast([128, 2048]), wr=[R_pgbc])
        P.op("dve", lambda e: e.memset(ones_b[:], 1.0), wr=[R_ob])
        P.dma("sp", "c8", preg_bc[:], row_d[0:1, R_PREG:R_PREG + 2048].to_broadcast([128, 2048]), wr=[R_pregbc])
        P.op("dve", lambda e: e.tensor_copy(out=ident_b[:], in_=ident[:]), rd=[R_id], wr=[R_idb])
        P.op("dve", lambda e: e.memset(cc[:, 0:1], -0.5), wr=[R_cc])
        P.op("dve", lambda e: e.memset(cc[:, 1:2], 0.5), wr=[R_cc])
        P.op("dve", lambda e: e.memset(cc[:, 2:3], EPS), wr=[R_cc])
        P.op("dve", lambda e: e.memset(cc[:, 3:4], 0.25), wr=[R_cc])
        P.op("dve", lambda e: e.memset(hst[:], 0.0), wr=R_hst)
        P.op("dve", lambda e: e.memset(halo[:], 0.0), wr=R_halo)

        R_win = [P.res(f"wsc_in{s}") for s in range(10)]
        R_winh = {(s_, h_): P.res(f"wsc_h{s_}_{h_}") for s_ in range(10) for h_ in range(2)}
        R_wout = [P.res(f"wsc_out{s}") for s in range(4)]
        R_wpg = [P.res(f"wsc_pg{s}") for s in range(4)]
        win_v = win_d.rearrange("(kc p) (s f) -> s p kc f", p=128, f=512)
        wout_v = wout_d.rearrange("(kc p) (s f) -> s p kc f", p=128, f=512)
        wpg_v = wpg_d.rearrange("(kc p) (s f) -> s p kc f", p=128, f=512)

        conv_toks = []

        def throttle():
            if len(conv_toks) >= 2:
                P.wait_all("pool", [conv_toks[-2]])

        win_h = win_d.rearrange("(kc p) (s a f) -> s a p kc f", p=128, a=2, f=256)

        def conv_in(s):
            if s in (2, 3):
                throttle()
                conv_toks.append(P.dma("pool", f"cv_in{s}", wsc_v[s - 2], win_v[s], wr=[R_win[s]]))
            else:
                for half in range(2):
                    throttle()
                    conv_toks.append(P.dma("pool", f"cv_in{s}_{half}", wsc_h[hidx(s, half)], win_h[s][half], wr=[R_winh[(s, half)]]))

        def conv_out(s):
            throttle()
            conv_toks.append(P.dma("pool", f"cv_out{s}", wsc_out[s], wout_v[s], wr=[R_wout[s]]))

        def conv_pg(s):
            throttle()
            conv_toks.append(P.dma("pool", f"cv_pg{s}", wsc_pg[s], wpg_v[s], wr=[R_wpg[s]]))

        import os
        SKIP = os.environ.get('MK_SKIP', '').split(',')
        if 'conv' not in SKIP:
            conv_in(6); conv_in(7)
        if 'wa' not in SKIP: throttle(); conv_toks.append(P.dma("pool", "cv_wa", wa_sb[:], wa_d.rearrange("h i j -> i h j"), wr=[R_wa]))
        if 'wa' not in SKIP: throttle(); conv_toks.append(P.dma("pool", "cv_wx", wx_sb[:], wx_d.rearrange("h i j -> i h j"), wr=[R_wx]))
        pending_conv = [lambda s=s: conv_in(s) for s in (8, 9, 0, 1, 2, 3, 4, 5)]
        pending_conv += [lambda: (throttle(), conv_toks.append(P.dma("pool", "cv_wpe", wpe_sb[:], wpe_d.rearrange("(kc p) f -> p kc f", p=128), wr=[R_wpe])))]
        pending_conv += [lambda s=s: conv_out(s) for s in range(4)] + [lambda s=s: conv_pg(s) for s in range(4)]

        def slab_src(kind, s):
            if kind == "in":
                assert s in (2, 3)
                return wsc_v[s - 2], R_win[s]
            if kind == "out":
                return wsc_out[s], R_wout[s]
            return wsc_pg[s], R_wpg[s]

        def get_slab(kind, s):
            src, rsrc = slab_src(kind, s)
            i = slab_i["i"] % 2
            slab_i["i"] += 1
            P.dma("sp", f"slab{i}", slab[i][:], src, rd=[rsrc], wr=[R_slab[i]])
            return slab[i], R_slab[i]

        def get_slab2(parts):
            i = slab_i["i"] % 2
            slab_i["i"] += 1
            view = slab[i][:].rearrange("p (a k2) f -> p a (k2 f)", a=2).rearrange("p a (k f) -> p a k f", f=256)
            for a_, (sidx, half) in enumerate(parts):
                P.dma("sp", f"slab{i}", view[:, a_, :, :], wsc_h[hidx(sidx, half)], rd=[R_winh[(sidx, half)]], wr=[R_slab[i]])
            return view, R_slab[i]

        def proj_fm2(view, rsl, a_, col, src, rsrc, hold=False):
            pt, rp = ps_next(hold)
            def f(e, pt=pt):
                ins = None
                for kc in range(KC):
                    ins = e.matmul(pt[:], lhsT=view[:, a_, kc, col:col + 128], rhs=src[:, kc, :], start=(kc == 0), stop=(kc == KC - 1))
                return ins
            P.op("pe", f, rd=[rsl] + rsrc, wr=[rp])
            return pt, rp

        if 'der' not in SKIP:
            P.op("dve", lambda e: e.tensor_scalar(out=der[:, 0:8], in0=cst[:, C_BA:C_BA + 8], scalar1=0.5, scalar2=None, op0=ALU.mult), rd=[R_cst], wr=[R_der])
            P.op("dve", lambda e: e.tensor_scalar(out=der[:, 8:16], in0=cst[:, C_BX:C_BX + 8], scalar1=0.5, scalar2=None, op0=ALU.mult), rd=[R_cst], wr=[R_der])
            lam = cst[:, C_LAM:C_LAM + 8]
            t0, t1, t2, t3 = der[:, 32:40], der[:, 40:48], small[:, 0:8], small[:, 8:16]
            R_sm = P.res("small")
            P.op("dve", lambda e: e.tensor_scalar(out=t0, in0=lam, scalar1=-1.0, scalar2=None, op0=ALU.mult), rd=[R_cst], wr=[R_der])
            P.op("dve", lambda e: e.tensor_tensor(out=t0, in0=t0, in1=lam, op=ALU.max), rd=[R_cst, R_der], wr=[R_der])
            P.op("act", lambda e: e.activation(out=t0, in_=t0, func=AF.Exp, scale=-1.0), rd=[R_der], wr=[R_der])
            P.op("dve", lambda e: e.tensor_scalar(out=t1, in0=t0, scalar1=1.0, scalar2=None, op0=ALU.add), rd=[R_der], wr=[R_der])
            P.op("act", lambda e: e.activation(out=t2, in_=t1, func=AF.Ln), rd=[R_der], wr=[R_sm])
            P.op("dve", lambda e: e.tensor_scalar(out=t1, in0=t1, scalar1=-1.0, scalar2=1e-30, op0=ALU.add, op1=ALU.max), rd=[R_der], wr=[R_der])
            P.op("dve", lambda e: e.reciprocal(out=t1, in_=t1), rd=[R_der], wr=[R_der])
            P.op("dve", lambda e: e.tensor_tensor(out=t1, in0=t1, in1=t0, op=ALU.mult), rd=[R_der], wr=[R_der])
            P.op("dve", lambda e: e.tensor_tensor(out=t2, in0=t2, in1=t1, op=ALU.mult), rd=[R_der, R_sm], wr=[R_sm])
            P.op("dve", lambda e: e.tensor_scalar(out=t3, in0=lam, scalar1=-1.0, scalar2=0.0, op0=ALU.mult, op1=ALU.max), rd=[R_cst], wr=[R_sm])
            P.op("dve", lambda e: e.tensor_tensor(out=t2, in0=t2, in1=t3, op=ALU.add), rd=[R_sm], wr=[R_sm])
            P.op("dve", lambda e: e.tensor_scalar(out=der[:, 16:24], in0=t2, scalar1=-8.0, scalar2=None, op0=ALU.mult), rd=[R_sm], wr=[R_der])
            P.op("dve", lambda e: e.tensor_scalar(out=der[:, 24:32], in0=t2, scalar1=-4.0, scalar2=None, op0=ALU.mult), rd=[R_sm], wr=[R_der])
            HBA, HBX, KF, KH = 0, 8, 16, 24

        if 'ws' not in SKIP:
            ws_f = arena[:, 0:1024].rearrange("p (h s) -> p h s", h=8)
            ws_b = ar_bf(2, 1).rearrange("p (h s) -> p h s", h=8)
            bs_bc = arena[:, 4 * 512:6 * 512]
            c32 = arena[:, 6 * 512:8 * 512]
            P.dma("sp", "c7", ws_f, ws_d.rearrange("h t s -> t h s"), wr=[R_ar[0], R_ar[1]])
            P.dma("sp", "c6", bs_bc, row_d[0:1, R_BS:R_BS + 1024].to_broadcast([128, 1024]), wr=[R_ar[4], R_ar[5]])
            for h in range(8):
                P.op("dve", lambda e, h=h: e.tensor_tensor(out=ws_b[:, h, :], in0=ws_f[:, h, :], in1=mask[:], op=ALU.mult),
                     rd=[R_mask, R_ar[0], R_ar[1]], wr=[R_ar[2]])
            for h2 in range(2):
                pt, rp = ps_next()
                ptb = pt[:].bitcast(BF16)
                def f(e, h2=h2, ptb=ptb):
                    ins = None
                    for q in range(4):
                        h = h2 * 4 + q
                        ins = e.transpose(out=ptb[:, q * 128:(q + 1) * 128], in_=ws_b[:, h, :], identity=ident_b[:])
                    return ins
                P.op("pe", f, rd=[R_ar[2], R_idb], wr=[rp])
                P.op("dve", lambda e, h2=h2, ptb=ptb: e.tensor_copy(out=wsT[:, h2 * 4:(h2 + 1) * 4, :], in_=ptb[:, 0:512].rearrange("p (h t) -> p h t", h=4)), rd=[rp], wr=[R_wsT])
            for h2 in range(2):
                pt2, rp2 = ps_next()
                P.op("pe", lambda e, h2=h2, pt2=pt2: e.matmul(pt2[:], lhsT=ones_b[:], rhs=wsT[:, h2 * 4:(h2 + 1) * 4, :].rearrange("p h t -> p (h t)"), start=True, stop=True),
                     rd=[R_ob, R_wsT], wr=[rp2])
                for q in range(4):
                    h = h2 * 4 + q
                    P.op("dve", lambda e, h=h, q=q, pt2=pt2: e.scalar_tensor_tensor(out=c32[:, h * 128:(h + 1) * 128], in0=pt2[:, q * 128:(q + 1) * 128], scalar=cst[:, C_LNB + h:C_LNB + h + 1],
                                                                         in1=bs_bc[:, h * 128:(h + 1) * 128], op0=ALU.mult, op1=ALU.add),
                         rd=[rp2, R_cst, R_ar[4], R_ar[5]], wr=[R_ar[6], R_ar[7]])
            chi = chl[:, 0, :, :].rearrange("p h t -> p (h t)")
            clo = chl[:, 1, :, :].rearrange("p h t -> p (h t)")
            P.op("dve", lambda e: e.tensor_copy(out=chi, in_=c32), rd=[R_ar[6], R_ar[7]], wr=[R_chl])
            P.op("dve", lambda e: e.tensor_tensor(out=clo, in0=c32, in1=chi, op=ALU.subtract), rd=[R_ar[6], R_ar[7], R_chl], wr=[R_chl])

        def rsqrt_pool(out_ap, in_ap, n, rd, wr):
            P.op("pool", lambda e: e.tensor_tensor(out=out_ap, in0=in_ap, in1=cc[:, 0:1].to_broadcast([128, n]), op=ALU.pow), rd=list(rd) + [R_cc], wr=wr)

        R_ss = P.res("ss4")
        ss4, ms4, rstd4 = small[:, 16:20], small[:, 20:24], small[:, 24:28]

        def transpose_to(tt, srcb, rsrc, dstbuf, Rdst):
            for half in range(2):
                pt, rp = ps_next()
                ptb = pt[:].bitcast(BF16)
                def ft(e, half=half, ptb=ptb):
                    ins = None
                    for q in range(8):
                        kc = half * 8 + q
                        ins = e.transpose(out=ptb[:, q * 128:(q + 1) * 128], in_=srcb[:, kc * 128:(kc + 1) * 128], identity=ident_b[:])
                    return ins
                P.op("pe", ft, rd=rsrc + [R_idb], wr=[rp])
                dst = dstbuf[:, half * 8:(half + 1) * 8, tt * 128:(tt + 1) * 128]
                srcv = ptb.rearrange("p (k t) -> p k t", k=8)
                wr = [Rdst[half * 8 + q] for q in range(8)]
                if half == 0:
                    P.op("act", lambda e, dst=dst, srcv=srcv: e.activation(out=dst, in_=srcv, func=AF.Copy), rd=[rp], wr=wr)
                else:
                    P.op("dve", lambda e, dst=dst, srcv=srcv: e.tensor_copy(out=dst, in_=srcv), rd=[rp], wr=wr)

        R_ssx = [P.res(f"ssx{tt}") for tt in range(NT)]

        def stage_xa(src_d, c):
            def load(tt):
                r0 = c * T + tt * 128
                P.dma("pool", f"xl{tt % 2}", xld[tt % 2][:], src_d[r0:r0 + 128, :], wr=[R_xld[tt % 2]])
            def norm(tt):
                P.op("act", lambda e: e.activation(out=xsb[tt][:], in_=xld[tt % 2][:], func=AF.Square, accum_out=ss4[:, tt:tt + 1]),
                     rd=[R_xld[tt % 2]], wr=[R_xsb[tt], R_ssx[tt]])
                P.op("dve", lambda e: e.tensor_scalar(out=ms4[:, tt:tt + 1], in0=ss4[:, tt:tt + 1], scalar1=1.0 / D, scalar2=EPS, op0=ALU.mult, op1=ALU.add), rd=[R_ssx[tt]], wr=[R_ssx[tt]])
                rsqrt_pool(rstd4[:, tt:tt + 1], ms4[:, tt:tt + 1], 1, [R_ssx[tt]], [R_ssx[tt]])
                P.op("dve", lambda e: e.scalar_tensor_tensor(out=xsb[tt][:], in0=xld[tt % 2][:], scalar=rstd4[:, tt:tt + 1], in1=preg_bc[:], op0=ALU.mult, op1=ALU.mult),
                     rd=[R_xld[tt % 2], R_ssx[tt], R_pregbc], wr=[R_xsb[tt]])
            return [lambda: (load(0), load(1), norm(0)), lambda: (load(2), norm(1)), lambda: (load(3), norm(2)), lambda: norm(3)]

        def run_steps(steps):
            for st in steps:
                st()

        def stage_xb(dstbuf, Rdst, tts=(0, 1, 2, 3)):
            for tt in tts:
                transpose_to(tt, xsb[tt][:], [R_xsb[tt]], dstbuf, Rdst)

        def proj_fm(sl, rsl, col, src, rsrc, hold=False):
            pt, rp = ps_next(hold)
            def f(e, pt=pt):
                ins = None
                for kc in range(KC):
                    ins = e.matmul(pt[:], lhsT=sl[:, kc, col:col + 128], rhs=src[:, kc, :], start=(kc == 0), stop=(kc == KC - 1))
                return ins
            P.op("pe", f, rd=[rsl] + rsrc, wr=[rp])
            return pt, rp

        NSET = 4

        def lset(s):
            b = 5 * s
            return dict(xcb=ar_bf(b + 0)[:, 0:512], tr=ar(b + 1), tiw=ar(b + 2), a=ar(b + 3), a2m=ar(b + 4),
                        Rxcb=R_ar[b + 0], Rtr=R_ar[b + 1], Rtiw=R_ar[b + 2], Ra=R_ar[b + 3], Ra2=R_ar[b + 4], xb=xbs[s], Rxb=R_xbs[s])

        def lru_front_a(j, s, xb_pt, xb_rp, act_heavy=False, phase=0):
            L = lset(s)
            xb, Rxb, xcb, Rxcb = L["xb"], L["Rxb"], L["xcb"], L["Rxcb"]
            cw = lambda k: cst[:, C_CW + j * 4 + k:C_CW + j * 4 + k + 1]
            if phase in (0, -1):
                P.op("dve", lambda e: e.tensor_copy(out=xb[:, 0:3], in_=halo[:, j, :]), rd=[R_halo[j]], wr=[Rxb])
            if phase in (0, 1):
                if act_heavy:
                    P.op("act", lambda e: e.activation(out=xb[:, 3:515], in_=xb_pt[:], func=AF.Copy), rd=[xb_rp], wr=[Rxb])
                else:
                    P.op("dve", lambda e: e.tensor_copy(out=xb[:, 3:515], in_=xb_pt[:]), rd=[xb_rp], wr=[Rxb])
                P.op("dve", lambda e: e.tensor_copy(out=halo[:, j, :], in_=xb[:, 512:515]), rd=[Rxb], wr=[R_halo[j]])
                P.op("act", lambda e: e.activation(out=xb_pt[:], in_=xb_pt[:], func=AF.Identity, scale=cw(3), bias=cst[:, C_CB + j:C_CB + j + 1]), rd=[xb_rp, R_cst, Rxb], wr=[xb_rp])
                for k in (0, 1, 2):
                    P.op("dve", lambda e, k=k: e.scalar_tensor_tensor(out=xb_pt[:], in0=xb[:, k:k + 512], scalar=cw(k), in1=xb_pt[:], op0=ALU.mult, op1=ALU.add),
                         rd=[Rxb, R_cst, xb_rp], wr=[xb_rp])
            if phase in (0, 2):
                if act_heavy:
                    P.op("act", lambda e: e.activation(out=xcb, in_=xb_pt[:], func=AF.Copy), rd=[xb_rp], wr=[Rxcb])
                else:
                    P.op("dve", lambda e: e.tensor_copy(out=xcb, in_=xb_pt[:]), rd=[xb_rp], wr=[Rxcb])

        def lru_front_b(j, s, xb_pt, xb_rp):
            L = lset(s)
            xcb, tr, tiw, a_t, a2m = L["xcb"], L["tr"], L["tiw"], L["a"], L["a2m"]
            Rxcb, Rtr, Rtiw, Ra, Ra2 = L["Rxcb"], L["Rtr"], L["Rtiw"], L["Ra"], L["Ra2"]
            zr, rzr = ps_next()
            P.op("pe", lambda e: e.matmul(zr[:], lhsT=wa_sb[:, j, :], rhs=xcb, start=True, stop=True), rd=[R_wa, Rxcb], wr=[rzr])
            zi, rzi = ps_next()
            P.op("pe", lambda e: e.matmul(zi[:], lhsT=wx_sb[:, j, :], rhs=xcb, start=True, stop=True), rd=[R_wx, Rxcb], wr=[rzi])
            col = lambda base: der[:, base + j:base + j + 1]
            P.op("act", lambda e: e.activation(out=tr, in_=zr[:], func=AF.Tanh, scale=0.5, bias=col(HBA)), rd=[rzr, R_der], wr=[Rtr])
            P.op("act", lambda e: e.activation(out=tiw, in_=zi[:], func=AF.Tanh, scale=0.5, bias=col(HBX)), rd=[rzi, R_der], wr=[Rtiw])
            P.op("dve", lambda e: e.scalar_tensor_tensor(out=tiw, in0=tiw, scalar=1.0, in1=xb_pt[:], op0=ALU.add, op1=ALU.mult), rd=[Rtiw, xb_rp], wr=[Rtiw])
            P.op("act", lambda e: e.activation(out=a_t, in_=tr, func=AF.Exp, scale=col(KH), bias=col(KH)), rd=[Rtr, R_der], wr=[Ra])
            P.op("act", lambda e: e.activation(out=a2m, in_=tr, func=AF.Exp, scale=col(KF), bias=col(KF)), rd=[Rtr, R_der], wr=[Ra2])
            P.op("dve", lambda e: e.tensor_scalar(out=a2m, in0=a2m, scalar1=0.99999994, scalar2=None, op0=ALU.min), rd=[Ra2], wr=[Ra2])

        def lru_back(j, s, first_mode, to_yab):
            L = lset(s)
            tr, tiw, a_t, a2m = L["tr"], L["tiw"], L["a"], L["a2m"]
            Rtr, Rtiw, Ra, Ra2 = L["Rtr"], L["Rtiw"], L["Ra"], L["Ra2"]
            P.op("act", lambda e: e.activation(out=a2m, in_=a2m, func=AF.Sqrt, scale=-0.25, bias=cc[:, 3:4]), rd=[Ra2, R_cc], wr=[Ra2])
            if first_mode == "force":
                P.op("dve", lambda e: e.memset(a2m[:, 0:1], 0.5), rd=[Ra2], wr=[Ra2])
            elif first_mode == "flag":
                tmpc = small[:, 32 + j:33 + j]
                P.op("dve", lambda e: e.tensor_scalar(out=tmpc, in0=a2m[:, 0:1], scalar1=-1.0, scalar2=0.5, op0=ALU.mult, op1=ALU.add), rd=[Ra2], wr=[R_sm])
                P.op("dve", lambda e: e.scalar_tensor_tensor(out=a2m[:, 0:1], in0=tmpc, scalar=cst[:, C_ISFIRST:C_ISFIRST + 1], in1=a2m[:, 0:1], op0=ALU.mult, op1=ALU.add),
                     rd=[Ra2, R_sm, R_cst], wr=[Ra2])
            P.op("dve", lambda e: e.tensor_tensor(out=tiw, in0=tiw, in1=a2m, op=ALU.mult), rd=[Rtiw, Ra2], wr=[Rtiw])
            if to_yab:
                h_t, Rh = yab[:, j, :], R_yab[j]
            else:
                h_t, Rh = tr, Rtr
            P.op("dve", lambda e: e.tensor_tensor_scan(out=h_t, data0=a_t, data1=tiw, initial=hst[:, j:j + 1], op0=ALU.mult, op1=ALU.add),
                 rd=[Ra, Rtiw, R_hst[j]], wr=[Rh])
            P.op("dve", lambda e: e.tensor_copy(out=hst[:, j:j + 1], in_=h_t[:, 511:512]), rd=[Rh], wr=[R_hst[j]])

        def norm_and_pack(ss_pt, ss_rp, gcol0, k0):
            rb = ss_pt[:]
            P.op("act", lambda e: e.activation(out=ss_pt[:], in_=ss_pt[:], func=AF.Sqrt, scale=1.0 / DG, bias=cc[:, 2:3]), rd=[ss_rp, R_cc], wr=[ss_rp])
            P.op("dve", lambda e: e.reciprocal(out=ss_pt[:], in_=ss_pt[:]), rd=[ss_rp], wr=[ss_rp])
            for j in range(8):
                P.op("dve", lambda e, j=j: e.scalar_tensor_tensor(out=ynT[:, k0 + j, :], in0=yab[:, j, :], scalar=cst[:, gcol0 + j:gcol0 + j + 1], in1=rb, op0=ALU.mult, op1=ALU.mult),
                     rd=[R_yab[j], R_cst, ss_rp], wr=[R_ynT[k0 + j]])

        def stage_lru(c, prepass, src, Rsrc, mid_hook=None):
            first_mode = None
            if c == 0:
                first_mode = "force" if prepass else "flag"
            slabs = {}
            ssb, rssb = psb[6], R_ps[6]
            deferred = []
            def flush():
                while deferred:
                    deferred.pop(0)()
            groups = [(0, 1, 2, 3), (4, 5, 6, 7)] if prepass else [(0, 1), (2, 3), (4, 5), (6, 7)]
            for pr, tiles in enumerate(groups):
                xbp = {}
                gbs = {}
                if prepass:
                    for j in tiles:
                        lru_front_a(j, j % NSET, None, None, act_heavy=True, phase=-1)
                    for j in tiles:
                        if j % 4 == 0:
                            slabs["xb"] = get_slab2([(6 + j // 4, 0), (6 + j // 4, 1)])
                        view, rsl = slabs["xb"]
                        xbp[j] = proj_fm2(view, rsl, (j % 4) // 2, (j % 2) * 128, src, Rsrc, hold=True)
                        lru_front_a(j, j % NSET, *xbp[j], act_heavy=True, phase=1)
                    for j in tiles:
                        lru_front_a(j, j % NSET, *xbp[j], act_heavy=True, phase=2)
                else:
                    view, rsl = get_slab2([(6 + pr // 2, pr % 2), (8 + pr // 2, pr % 2)])
                    for j in tiles:
                        xbp[j] = proj_fm2(view, rsl, 0, (j % 2) * 128, src, Rsrc, hold=True)
                        lru_front_a(j, j % NSET, *xbp[j])
                    for j in tiles:
                        gbs[j] = proj_fm2(view, rsl, 1, (j % 2) * 128, src, Rsrc, hold=True)
                flush()
                for j in tiles:
                    lru_front_b(j, j % NSET, *xbp[j])
                    ps_release(xbp[j][0])
                for j in tiles:
                    lru_back(j, j % NSET, first_mode, not prepass)
                if not prepass:
                    for j in tiles:
                        gb_pt, gb_rp = gbs[j]
                        L = lset(j % NSET)
                        tg, Rtg = L["tr"], L["Rtr"]
                        s2 = j % 2
                        P.op("act", lambda e, tg=tg, gb_pt=gb_pt: e.activation(out=tg, in_=gb_pt[:], func=AF.Tanh, scale=0.5), rd=[gb_rp], wr=[Rtg])
                        P.op("dve", lambda e, tg=tg, gb_pt=gb_pt: e.scalar_tensor_tensor(out=gb_pt[:], in0=tg, scalar=1.0, in1=gb_pt[:], op0=ALU.add, op1=ALU.mult), rd=[Rtg, gb_rp], wr=[gb_rp])
                        P.op("dve", lambda e, j=j, gb_pt=gb_pt: e.scalar_tensor_tensor(out=yab[:, j, :], in0=yab[:, j, :], scalar=0.5, in1=gb_pt[:], op0=ALU.mult, op1=ALU.mult), rd=[R_yab[j], gb_rp], wr=[R_yab[j]])
                        if pr == 3:
                            P.op("act", lambda e, j=j, s2=s2: e.activation(out=sqb[s2][:], in_=yab[:, j, :], func=AF.Square), rd=[R_yab[j]], wr=[R_sqb[s2]])
                        else:
                            P.op("pool", lambda e, j=j, s2=s2: e.tensor_tensor(out=sqb[s2][:], in0=yab[:, j, :], in1=yab[:, j, :], op=ALU.mult), rd=[R_yab[j]], wr=[R_sqb[s2]])
                        ps_release(gb_pt)
                        deferred.append(lambda j=j, s2=s2: P.op("pe", lambda e: e.matmul(ssb[:], lhsT=ones_b[:], rhs=sqb[s2][:], start=(j == 0), stop=(j == 7)), rd=[R_ob, R_sqb[s2]], wr=[rssb]))
                if mid_hook is not None and pr == len(groups) // 2 - 1:
                    mid_hook()
            if prepass:
                flush()
                return None
            def finish():
                flush()
                norm_and_pack(ssb, rssb, C_GB, 8)
            return finish

        gm = {}

        def stage_gmlp_v(c, after_first=None):
            v_ln = ar_bf(4, 4).rearrange("p (t f) -> p t f", t=4)
            R_vln = [R_ar[4 + tt] for tt in range(4)]
            stats = small[:, 40:52]
            mv = small[:, 52:54]
            rv = small[:, 54:56]
            R_st = P.res("stats") if "st" not in R_small else R_small["st"]
            R_small["st"] = R_st
            slv = [get_slab("in", 2), get_slab("in", 3)]
            for tt in range(NT):
                for half in range(2):
                    sl, rsl = slv[half]
                    pt, rp = ps_next()
                    def f(e, pt=pt, sl=sl, tt=tt):
                        ins = None
                        for kc in range(KC):
                            ins = e.matmul(pt[:], lhsT=xT[:, kc, tt * 128:(tt + 1) * 128], rhs=sl[:, kc, :], start=(kc == 0), stop=(kc == KC - 1))
                        return ins
                    P.op("pe", f, rd=[rsl] + R_xT, wr=[rp])
                    gslot = (tt % 2) * 2 + half
                    P.op("act", lambda e, pt=pt, gslot=gslot: e.activation(out=ar(gslot), in_=pt[:], func=AF.Gelu_apprx_tanh), rd=[rp], wr=[R_ar[gslot]])
                g0 = (tt % 2) * 2
                gv = arena[:, g0 * 512:(g0 + 2) * 512]
                P.op("dve", lambda e, gv=gv: e.bn_stats(out=stats[:, 0:6], in_=gv[:, 0:512]), rd=[R_ar[g0]], wr=[R_st])
                P.op("dve", lambda e, gv=gv: e.bn_stats(out=stats[:, 6:12], in_=gv[:, 512:1024]), rd=[R_ar[g0 + 1]], wr=[R_st])
                P.op("dve", lambda e: e.bn_aggr(out=mv, in_=stats), rd=[R_st], wr=[R_st])
                P.op("dve", lambda e: e.tensor_scalar(out=rv[:, 0:1], in0=mv[:, 1:2], scalar1=EPS, scalar2=None, op0=ALU.add), rd=[R_st], wr=[R_st])
                rsqrt_pool(rv[:, 0:1], rv[:, 0:1], 1, [R_st], [R_st])
                P.op("dve", lambda e: e.scalar_tensor_tensor(out=rv[:, 1:2], in0=mv[:, 0:1], scalar=-1.0, in1=rv[:, 0:1], op0=ALU.mult, op1=ALU.mult), rd=[R_st], wr=[R_st])
                P.op("act", lambda e, gv=gv: e.activation(out=gv, in_=gv, func=AF.Identity, scale=rv[:, 0:1], bias=rv[:, 1:2]), rd=[R_st, R_ar[g0], R_ar[g0 + 1]], wr=[R_ar[g0], R_ar[g0 + 1]])
                P.op("dve", lambda e, gv=gv, tt=tt: e.tensor_tensor(out=v_ln[:, tt, :], in0=gv, in1=g_bc[:], op=ALU.mult), rd=[R_ar[g0], R_ar[g0 + 1], R_gbc], wr=[R_vln[tt]])
                if tt == 1 and after_first is not None:
                    after_first()
            gm['v_ln'] = v_ln; gm['R_vln'] = R_vln

        def stage_gmlp_heads(c, steps=()):
            steps = list(steps)
            v_ln, R_vln = gm['v_ln'], gm['R_vln']
            ssa, rssa = psb[7], R_ps[7]
            slabs = {}
            hdef = []
            def hproj(j):
                if j % 2 == 0:
                    slabs["p"] = get_slab2([(0 + j // 4, (j % 4) // 2), (4 + j // 4, (j % 4) // 2)])
                s = j % 2
                b = 8 + 4 * s
                tg, ug, sga2, usg = ar(b), ar(b + 1), ar(b + 2), ar(b + 3)
                Rtg, Rug, Rsga, Rusg = R_ar[b], R_ar[b + 1], R_ar[b + 2], R_ar[b + 3]
                view, rsl = slabs["p"]
                ga_pt, ga_rp = proj_fm2(view, rsl, 1, (j % 2) * 128, xT, R_xT)
                while hdef:
                    hdef.pop(0)()
                P.op("act", lambda e: e.activation(out=tg, in_=ga_pt[:], func=AF.Tanh, scale=0.5), rd=[ga_rp], wr=[Rtg])
                P.op("dve", lambda e: e.scalar_tensor_tensor(out=sga2, in0=tg, scalar=1.0, in1=ga_pt[:], op0=ALU.add, op1=ALU.mult), rd=[Rtg, ga_rp], wr=[Rsga])
                u_pt, u_rp = proj_fm2(view, rsl, 0, (j % 2) * 128, xT, R_xT)
                P.op("act", lambda e: e.activation(out=ug, in_=u_pt[:], func=AF.Gelu_apprx_tanh), rd=[u_rp], wr=[Rug])
                P.op("dve", lambda e: e.scalar_tensor_tensor(out=usg, in0=ug, scalar=0.5, in1=sga2, op0=ALU.mult, op1=ALU.mult), rd=[Rug, Rsga], wr=[Rusg])

            def hmix(j):
                s = j % 2
                b = 8 + 4 * s
                usg, Rusg = ar(b + 3), R_ar[b + 3]
                mp, rmp = ps_next()
                def fm(e):
                    ins = None
                    for cq in range(4):
                        e.matmul(mp[:, cq * 128:(cq + 1) * 128], lhsT=ident_b[:], rhs=chl[:, 0, j, :], start=True, stop=False)
                        e.matmul(mp[:, cq * 128:(cq + 1) * 128], lhsT=ident_b[:], rhs=chl[:, 1, j, :], start=False, stop=False)
                        ins = e.matmul(mp[:, cq * 128:(cq + 1) * 128], lhsT=v_ln[:, cq, j * 128:(j + 1) * 128], rhs=wsT[:, j, :], start=False, stop=True)
                    return ins
                P.op("pe", fm, rd=[R_chl, R_idb, R_wsT] + R_vln, wr=[rmp])
                P.op("dve", lambda e: e.tensor_tensor(out=yab[:, j, :], in0=mp[:], in1=usg, op=ALU.mult), rd=[rmp, Rusg], wr=[R_yab[j]])
                P.op("act", lambda e: e.activation(out=sqb[s][:], in_=yab[:, j, :], func=AF.Square), rd=[R_yab[j]], wr=[R_sqb[s]])
                if j % 2 == 1 and steps:
                    steps.pop(0)()
                hdef.append(lambda: P.op("pe", lambda e: e.matmul(ssa[:], lhsT=ones_b[:], rhs=sqb[s][:], start=(j == 0), stop=(j == 7)), rd=[R_ob, R_sqb[s]], wr=[rssa]))

            hproj(0)
            for j in range(8):
                if j + 1 < 8:
                    hproj(j + 1)
                hmix(j)
            while hdef:
                hdef.pop(0)()
            norm_and_pack(ssa, rssa, C_GA, 0)

        part = small[:, 56:60]
        R_po = P.res("po")
        po16 = sb("po16", [128, 16])
        sso, rso = sb("sso", [128, 4]), sb("rso", [128, 4])

        def o_sb(tt):
            if tt < 2:
                return yab[:, tt * 4:(tt + 1) * 4, :].rearrange("p a b -> p (a b)"), R_yab[tt * 4:(tt + 1) * 4]
            return arena[:, (tt - 2) * 2048:(tt - 1) * 2048], R_ar[(tt - 2) * 4:(tt - 1) * 4]

        def stage_out(c, mid=None):
            sqj = ar_bf(16, 1)[:, 0:512]
            res_load(c, 0, "sp"); res_load(c, 1, "sp")
            for nb in range(4):
                sl, rsl = get_slab("out", nb)
                pre = {}
                if nb == 0:
                    for tt in range(NT):
                        pt, rp = ps_next(hold=True)
                        def f0(e, pt=pt, sl=sl, tt=tt):
                            ins = None
                            for kc in range(8, KC):
                                ins = e.matmul(pt[:], lhsT=ynT[:, kc, tt * 128:(tt + 1) * 128], rhs=sl[:, kc, :], start=(kc == 8), stop=False)
                            return ins
                        P.op("pe", f0, rd=[rsl] + R_ynT[8:], wr=[rp])
                        pre[tt] = (pt, rp)
                for tt in range(NT):
                    if nb == 0:
                        pt, rp = pre[tt]
                        def f(e, pt=pt, sl=sl, tt=tt):
                            ins = None
                            for kc in range(8):
                                ins = e.matmul(pt[:], lhsT=ynT[:, kc, tt * 128:(tt + 1) * 128], rhs=sl[:, kc, :], start=False, stop=(kc == 7))
                            return ins
                        P.op("pe", f, rd=[rsl] + R_ynT[:8], wr=[rp])
                        ps_release(pt)
                    else:
                        pt, rp = ps_next()
                        def f(e, pt=pt, sl=sl, tt=tt):
                            ins = None
                            for kc in range(KC):
                                ins = e.matmul(pt[:], lhsT=ynT[:, kc, tt * 128:(tt + 1) * 128], rhs=sl[:, kc, :], start=(kc == 0), stop=(kc == KC - 1))
                            return ins
                        P.op("pe", f, rd=[rsl] + R_ynT, wr=[rp])
                    osb, Rosb = o_sb(tt)
                    P.op("dve", lambda e, pt=pt, osb=osb, nb=nb: e.tensor_tensor(out=osb[:, nb * 512:(nb + 1) * 512], in0=pt[:], in1=pg_bc[:, nb * 512:(nb + 1) * 512], op=ALU.mult), rd=[rp, R_pgbc], wr=[Rosb[nb]])
                    P.op("act", lambda e, pt=pt, tt=tt, nb=nb: e.activation(out=sqj, in_=pt[:], func=AF.Square, accum_out=po16[:, tt * 4 + nb:tt * 4 + nb + 1]), rd=[rp], wr=[R_ar[16], R_po])
            if mid is not None:
                mid()
            P.op("dve", lambda e: e.reduce_sum(out=sso[:], in_=po16[:].rearrange("p (t n) -> p t n", t=4), axis=AX.X), rd=[R_po], wr=[R_po])
            P.op("dve", lambda e: e.tensor_scalar(out=sso[:], in0=sso[:], scalar1=1.0 / D, scalar2=EPS, op0=ALU.mult, op1=ALU.add), rd=[R_po], wr=[R_po])
            rsqrt_pool(rso[:], sso[:], 4, [R_po], [R_po])

        def res_load(c, tt, eng):
            r0 = c * T + tt * 128
            P.dma(eng, f"xr{tt % 2}", xld[tt % 2][:], x_d[r0:r0 + 128, :], wr=[R_xld[tt % 2]])

        def stage_h1T(c):
            for tt in range(NT):
                s2 = 16 + 2 * (tt % 2)
                hb = ar_bf(s2, 2)
                rs_ = [R_ar[s2], R_ar[s2 + 1]]
                osb, Rosb = o_sb(tt)
                for hf in range(2):
                    cs = slice(hf * 1024, (hf + 1) * 1024)
                    P.op("dve", lambda e, osb=osb, tt=tt, cs=cs: e.scalar_tensor_tensor(out=osb[:, cs], in0=osb[:, cs], scalar=rso[:, tt:tt + 1], in1=xld[tt % 2][:, cs], op0=ALU.mult, op1=ALU.add),
                         rd=Rosb[2 * hf:2 * hf + 2] + [R_po, R_xld[tt % 2]], wr=Rosb[2 * hf:2 * hf + 2])
                if tt + 2 < NT:
                    res_load(c, tt + 2, "act")
                P.op("act", lambda e, osb=osb, hb=hb: e.activation(out=hb[:, 0:1024], in_=osb[:, 0:1024], func=AF.Copy), rd=Rosb[0:2], wr=[rs_[0]])
                P.op("dve", lambda e, osb=osb, hb=hb: e.tensor_copy(out=hb[:, 1024:2048], in_=osb[:, 1024:2048]), rd=Rosb[2:4], wr=[rs_[1]])
                transpose_to(tt, hb, rs_, ynT, R_ynT)

        def stage_final(c):
            p_sb = arena[:, 8 * 512:10 * 512].rearrange("p (t f) -> p t f", t=4)
            pT = ar_bf(10, 1).rearrange("p (k t) -> p k t", k=2)
            P.dma("sp", "pld", p_sb, p_d[c * T:(c + 1) * T, :].rearrange("(t p) f -> p t f", p=128), wr=[R_ar[8], R_ar[9]])
            p_b = ar_bf(15, 1).rearrange("p (t f) -> p t f", t=4)
            P.op("dve", lambda e: e.tensor_copy(out=p_b, in_=p_sb), rd=[R_ar[8], R_ar[9]], wr=[R_ar[15]])
            for kc in range(2):
                pt, rp = ps_next()
                ptb = pt[:].bitcast(BF16)
                def ft(e, kc=kc, ptb=ptb):
                    ins = None
                    for tt in range(NT):
                        ins = e.transpose(out=ptb[:, tt * 128:(tt + 1) * 128], in_=p_b[:, tt, kc * 128:(kc + 1) * 128], identity=ident_b[:])
                    return ins
                P.op("pe", ft, rd=[R_ar[15], R_idb], wr=[rp])
                P.op("act", lambda e, kc=kc, ptb=ptb: e.activation(out=pT[:, kc, :], in_=ptb[:, 0:512], func=AF.Copy), rd=[rp], wr=[R_ar[10]])
            toks = []
            for nb in range(4):
                sl, rsl = get_slab("pg", nb)
                for tt in range(NT):
                    s = tt % 2
                    tgf, q = ar(11 + s), ar(13 + s)
                    Rtgf, Rq = R_ar[11 + s], R_ar[13 + s]
                    gp, rgp = ps_next()
                    def f(e, gp=gp, sl=sl, tt=tt):
                        ins = None
                        for kc in range(KC):
                            ins = e.matmul(gp[:], lhsT=ynT[:, kc, tt * 128:(tt + 1) * 128], rhs=sl[:, kc, :], start=(kc == 0), stop=(kc == KC - 1))
                        return ins
                    P.op("pe", f, rd=[rsl] + R_ynT, wr=[rgp])
                    pp, rpp = ps_next()
                    def f2(e, pp=pp, tt=tt, nb=nb):
                        ins = None
                        for kc in range(2):
                            ins = e.matmul(pp[:], lhsT=pT[:, kc, tt * 128:(tt + 1) * 128], rhs=wpe_sb[:, kc, nb * 512:(nb + 1) * 512], start=(kc == 0), stop=(kc == 1))
                        return ins
                    P.op("pe", f2, rd=[R_wpe, R_ar[10]], wr=[rpp])
                    sl_ = slice(nb * 512, (nb + 1) * 512)
                    P.op("act", lambda e, gp=gp, tgf=tgf: e.activation(out=tgf, in_=gp[:], func=AF.Tanh, scale=0.5), rd=[rgp], wr=[Rtgf])
                    P.op("dve", lambda e, tgf=tgf, q=q, pp=pp: e.scalar_tensor_tensor(out=q, in0=tgf, scalar=1.0, in1=pp[:], op0=ALU.add, op1=ALU.mult), rd=[Rtgf, rpp], wr=[Rq])
                    osb, Rosb = o_sb(tt)
                    P.op("dve", lambda e, q=q, osb=osb, sl_=sl_: e.scalar_tensor_tensor(out=osb[:, sl_], in0=q, scalar=0.5, in1=osb[:, sl_], op0=ALU.mult, op1=ALU.add), rd=[Rq, Rosb[nb]], wr=[Rosb[nb]])
                    r0 = c * T + tt * 128
                    toks.append(P.dma("pool", f"st{tt}_{nb}", out_d[r0:r0 + 128, sl_], osb[:, sl_], rd=[Rosb[nb]]))
            return toks

        def run_pending(n):
            for _ in range(n):
                if pending_conv:
                    pending_conv.pop(0)()

        jobs = ([("pre", c) for c in range(NCH)] if do_prepass else []) + [("main", c) for c in range(NCH)]
        def buf_of(i):
            kind, c = jobs[i]
            if kind == "pre" and (NCH - 1 - c) % 2 == 0:
                return ynT, R_ynT
            return xT, R_xT
        def xa(i):
            kind, c = jobs[i]
            return stage_xa(xp_d if kind == "pre" else x_d, c)
        def xb_(i):
            stage_xb(*buf_of(i))
        all_toks = []
        run_steps(xa(0)); xb_(0)
        for i, (kind, c) in enumerate(jobs):
            nxt = i + 1 if i + 1 < len(jobs) else None
            src, Rsrc = buf_of(i)
            if kind == "pre":
                if nxt is not None:
                    run_steps(xa(nxt))
                stage_lru(c, True, src, Rsrc, mid_hook=(lambda nxt=nxt: xb_(nxt)) if nxt is not None else None)
                run_pending(3)
                if c == NCH - 1:
                    hp = cst[:, C_HASPREV:C_HASPREV + 1]
                    P.op("dve", lambda e: e.tensor_scalar(out=hst[:], in0=hst[:], scalar1=hp, scalar2=None, op0=ALU.mult), rd=R_hst + [R_cst], wr=R_hst)
                    P.op("dve", lambda e: e.tensor_scalar(out=halo[:].rearrange("p a b -> p (a b)"), in0=halo[:].rearrange("p a b -> p (a b)"), scalar1=hp, scalar2=None, op0=ALU.mult), rd=R_halo + [R_cst], wr=R_halo)
            else:
                run_pending(100)
                lru_finish = stage_lru(c, False, src, Rsrc)
                stage_gmlp_v(c, after_first=lru_finish)
                steps = xa(nxt) if nxt is not None else []
                stage_gmlp_heads(c, steps)
                if nxt is not None:
                    stage_xb(*buf_of(nxt), tts=(0, 1))
                stage_out(c, mid=(lambda nxt=nxt: stage_xb(*buf_of(nxt), tts=(2, 3))) if nxt is not None else None)
                stage_h1T(c)
                all_toks += stage_final(c)
        P.wait_all("sp", all_toks)
        P.emit()
    return nc


def pack_inputs(inp, NCH):
    x = inp["x"]; p = inp["p"][0]
    B, S, _ = x.shape
    NTOK = NCH * T
    assert S == 2 * NTOK
    f = lambda a: np.ascontiguousarray(a, dtype=np.float32)
    cst = np.zeros((128, NCST), np.float32)
    cst[:, C_PREG:C_PREG + 16] = inp["pre_g"][0].reshape(16, 128).T
    cw = inp["conv_w"][0][:, 0, :]
    cst[:, C_CW:C_CW + 32] = cw.reshape(4, 8, 128).transpose(2, 1, 0).reshape(128, 32)
    col8 = lambda v: v.reshape(8, 128).T
    cst[:, C_CB:C_CB + 8] = col8(inp["conv_b"][0])
    cst[:, C_BA:C_BA + 8] = col8(inp["b_a"][0].reshape(-1))
    cst[:, C_BX:C_BX + 8] = col8(inp["b_x"][0].reshape(-1))
    cst[:, C_LAM:C_LAM + 8] = col8(inp["lam"][0])
    cst[:, C_GA:C_GA + 8] = col8(inp["gmlp_out_g"][0])
    cst[:, C_GB:C_GB + 8] = col8(inp["lru_out_g"][0])
    rowv = np.zeros((1, NROW), np.float32)
    rowv[0, R_LNG:R_LNG + 1024] = inp["gmlp_ln_g"][0]
    rowv[0, R_LNB:R_LNB + 1024] = inp["gmlp_ln_b"][0]
    rowv[0, R_POSTG:R_POSTG + 2048] = inp["post_g"][0]
    rowv[0, R_BS:R_BS + 1024] = inp["gmlp_bs"][0].reshape(-1)
    rowv[0, R_PREG:R_PREG + 2048] = inp["pre_g"][0]
    cst[:, C_LNB:C_LNB + 8] = col8(inp["gmlp_ln_b"][0])
    common = {
        "w_in": f(inp["w_in"][0]), "w_out": f(inp["w_out"][0]), "w_pg": f(inp["w_pg"][0]), "w_pe": f(inp["w_pe"][0]),
        "w_a": f(inp["w_a"][0]), "w_x": f(inp["w_x"][0]), "gmlp_ws": f(inp["gmlp_ws"][0]),
        "rowv": rowv, "ident": np.eye(128, dtype=np.float32), "mask": np.tril(np.ones((128, 128), np.float32)),
    }
    maps = []
    for b in range(B):
        for half in range(2):
            c = cst.copy()
            c[:, C_HASPREV] = float(half)
            c[:, C_ISFIRST] = 1.0 - float(half)
            m = dict(common)
            m["cst"] = c
            m["x"] = f(x[b, half * NTOK:(half + 1) * NTOK])
            m["xprev"] = f(x[b, 0:NTOK]) if half == 1 else np.zeros((NTOK, D), np.float32)
            m["p"] = f(p[b, half * NTOK:(half + 1) * NTOK])
            maps.append(m)
    return maps

def unpack(results, B, NCH):
    NTOK = NCH * T
    out = np.zeros((B, 2 * NTOK, D), np.float32)
    for b in range(B):
        for half in range(2):
            out[b, half * NTOK:(half + 1) * NTOK] = results[b * 2 + half]["out"]
    return out


from concourse.bass_utils import run_bass_kernel_spmd

NCH_FULL = 8


def kernel(**inputs):
    inp = {k: np.asarray(v) for k, v in inputs.items()}
    B = inp["x"].shape[0]
    nc = bass.Bass("TRN2", target_bir_lowering=False)
    build_program(nc, NCH_FULL)
    maps = pack_inputs(inp, NCH_FULL)
    res = run_bass_kernel_spmd(nc, maps, core_ids=list(range(2 * B)))
    return unpack(res.results, B, NCH_FULL)
```
